# Optimizing a Trainium2 kernel written in Bass

```python
import jax, jax.numpy as jnp
from jax import lax
import numpy as np

D_MODEL = 4096
BATCH = 4
SEQ = 2048
DEPTH = 2
DEC_BATCH = 8
DEC_SEQ = 8
PAST_LEN = 16384
PAGE_SIZE = 128

N_MIXERS = 2
N_ATTN_LAYERS = (DEPTH + 1) // 2
N_CONV_LAYERS = DEPTH // 2
HEAD_DIM = 128
HEADS_PER_GROUP = 16
DILATED_GROUPS = ((128, 1), (512, 4), (2048, 16))
N_GROUPS = len(DILATED_GROUPS)
ATTN_WIDTH = HEADS_PER_GROUP * HEAD_DIM
QKV_WIDTH = N_GROUPS * 3 * ATTN_WIDTH
Q_BLOCK = 128
CONF_KERNEL = 31
FFN_KERNEL = 3
D_FF = 11008
RMS_EPS = 1e-6
LN_EPS = 1e-5

kernel_name = 'hybrid_dilated_attn_conformer_convffn_step'


def rms_norm(x):
    xf = x.astype(jnp.float32)
    return (xf * lax.rsqrt(jnp.mean(xf * xf, axis=-1, keepdims=True) + RMS_EPS)).astype(x.dtype)


def layer_norm(x, g, b):
    xf = x.astype(jnp.float32)
    mu = jnp.mean(xf, axis=-1, keepdims=True)
    var = jnp.mean(jnp.square(xf - mu), axis=-1, keepdims=True)
    y = (xf - mu) * lax.rsqrt(var + LN_EPS) * g.astype(jnp.float32) + b.astype(jnp.float32)
    return y.astype(x.dtype)


def modulate(x, shift, scale):
    return rms_norm(x) * (1 + scale[:, None, :]) + shift[:, None, :]


def adaln_params(c, w, b):
    return jnp.split(jax.nn.silu(c) @ w + b, 6, axis=-1)


def depthwise_conv(x, w):
    return lax.conv_general_dilated(x, w[:, None, :].astype(x.dtype), window_strides=(1,), padding='VALID',
                                    dimension_numbers=('NWC', 'WIO', 'NWC'),
                                    feature_group_count=x.shape[-1])


def softmax_attend(s, v, spec):
    m = jnp.max(s, axis=-1, keepdims=True)
    p = jnp.exp(s - m)
    den = jnp.sum(p, axis=-1, keepdims=True)
    o = jnp.einsum(spec, p / den, v.astype(jnp.float32))
    return o, (m + jnp.log(den))[..., 0]


def dilated_attn_prompt(q, k, v, dilation, n_steps):
    B, S, H, Dh = q.shape
    L = S // dilation
    nb = -(-L // Q_BLOCK)
    Lp = nb * Q_BLOCK

    def to_sub(a):
        a = a.reshape(B, L, dilation, H, Dh).swapaxes(1, 2).reshape(B * dilation, L, H, Dh)
        return jnp.pad(a, ((0, 0), (0, Lp - L), (0, 0), (0, 0)))

    def from_sub(a):
        rest = a.shape[2:]
        return a[:, :L].reshape((B, dilation, L) + rest).swapaxes(1, 2).reshape((B, S) + rest)

    def band(a):
        ab = a.reshape(-1, nb, Q_BLOCK, H, Dh)
        prev = jnp.pad(ab, ((0, 0), (1, 0), (0, 0), (0, 0), (0, 0)))[:, :-1]
        return jnp.concatenate([prev, ab], axis=2)

    qb = to_sub(q).reshape(-1, nb, Q_BLOCK, H, Dh)
    kb, vb = band(to_sub(k)), band(to_sub(v))
    s = jnp.einsum('bnqhd,bnkhd->bnhqk', qb, kb, preferred_element_type=jnp.float32) * (Dh ** -0.5)
    qi = jnp.arange(Q_BLOCK)[:, None]
    ki = jnp.arange(2 * Q_BLOCK)[None, :]
    dist = qi + Q_BLOCK - ki
    key_idx = jnp.arange(nb)[:, None, None] * Q_BLOCK - Q_BLOCK + ki[None]
    valid = (dist >= 0) & (dist <= n_steps) & (key_idx >= 0)
    s = jnp.where(valid[None, :, None], s, -jnp.inf)
    o, lse = softmax_attend(s, vb, 'bnhqk,bnkhd->bnqhd')
    o = o.reshape(-1, Lp, H, Dh)
    lse = lse.swapaxes(2, 3).reshape(-1, Lp, H)
    return from_sub(o), from_sub(lse)


def dilated_attn_sample(q, k_all, v_all, buf_len, dilation, n_steps):
    T = q.shape[1]
    idx = buf_len + jnp.arange(T)[:, None] - dilation * jnp.arange(n_steps + 1)[None, :]
    valid = idx >= 0
    idx = jnp.maximum(idx, 0)
    kg = k_all[:, idx]
    vg = v_all[:, idx]
    s = jnp.einsum('bthd,btjhd->bhtj', q, kg, preferred_element_type=jnp.float32) * (q.shape[-1] ** -0.5)
    s = jnp.where(valid[None, None], s, -jnp.inf)
    o, lse = softmax_attend(s, vg, 'bhtj,btjhd->bthd')
    return o, lse.swapaxes(1, 2)


def dilated_mixer(h, w_qkv, w_o, caches):
    B, S, _ = h.shape
    qkv = (h @ w_qkv).reshape(B, S, N_GROUPS, 3, HEADS_PER_GROUP, HEAD_DIM)
    outs, lses, new_states = [], [], []
    for g, (window, dilation) in enumerate(DILATED_GROUPS):
        q, k, v = qkv[:, :, g, 0], qkv[:, :, g, 1], qkv[:, :, g, 2]
        n_steps = window // dilation
        if caches is None:
            o, lse = dilated_attn_prompt(q, k, v, dilation, n_steps)
            keep = min(window, S)
            new_states.append(jnp.stack([k[:, S - keep:], v[:, S - keep:]], axis=2))
        else:
            buf = caches[g]
            wb = buf.shape[1]
            k_all = jnp.concatenate([buf[:, :, 0].astype(k.dtype), k], axis=1)
            v_all = jnp.concatenate([buf[:, :, 1].astype(v.dtype), v], axis=1)
            o, lse = dilated_attn_sample(q, k_all, v_all, wb, dilation, n_steps)
            new_states.append(jnp.stack([k_all[:, -wb:], v_all[:, -wb:]], axis=2))
        outs.append(o)
        lses.append(lse)
    wts = jax.nn.softmax(jnp.stack(lses), axis=0)
    o = jnp.einsum('gbsh,gbshd->bshd', wts, jnp.stack(outs))
    return o.reshape(B, S, ATTN_WIDTH).astype(h.dtype) @ w_o, new_states


def conformer_conv_module(h, w_pw1, b_pw1, w_dw, b_dw, ln_g, ln_b, w_pw2, b_pw2, buf):
    B, S, D = h.shape
    a, gt = jnp.split(h @ w_pw1 + b_pw1, 2, axis=-1)
    u = a * jax.nn.sigmoid(gt)
    ctx = jnp.zeros((B, CONF_KERNEL - 1, D), u.dtype) if buf is None else buf.astype(u.dtype)
    u_all = jnp.concatenate([ctx, u], axis=1)
    y = depthwise_conv(u_all, w_dw) + b_dw
    y = jax.nn.silu(layer_norm(y, ln_g, ln_b))
    return y @ w_pw2 + b_pw2, u_all[:, -(CONF_KERNEL - 1):]


def conv_ffn(h, w_up, w_dw, w_down, buf):
    B = h.shape[0]
    u = h @ w_up
    ctx = jnp.zeros((B, FFN_KERNEL - 1, u.shape[-1]), u.dtype) if buf is None else buf.astype(u.dtype)
    u_all = jnp.concatenate([ctx, u], axis=1)
    gate, val = jnp.split(depthwise_conv(u_all, w_dw), 2, axis=-1)
    return (jax.nn.silu(gate) * val) @ w_down, u_all[:, -(FFN_KERNEL - 1):]


def trunk(x, c, kv_caches, conv_states, ffn_states, params):
    (w_ada, b_ada, w_qkv, w_o, w_pw1, b_pw1, w_dw, b_dw, ln_g, ln_b, w_pw2, b_pw2,
     w_up, w_ffn_dw, w_down, final_norm_g) = params
    new_kv = [[] for _ in range(N_GROUPS)]
    new_conv, new_ffn = [], []
    for i in range(DEPTH):
        sh1, sc1, gt1, sh2, sc2, gt2 = adaln_params(c, w_ada[i], b_ada[i])
        h = modulate(x, sh1, sc1)
        j = i // N_MIXERS
        if i % N_MIXERS == 0:
            caches = None if kv_caches is None else [cache[j] for cache in kv_caches]
            mix, states = dilated_mixer(h, w_qkv[j], w_o[j], caches)
            for lst, st in zip(new_kv, states):
                lst.append(st)
        else:
            buf = None if conv_states is None else conv_states[j]
            mix, st = conformer_conv_module(h, w_pw1[j], b_pw1[j], w_dw[j], b_dw[j], ln_g[j], ln_b[j],
                                            w_pw2[j], b_pw2[j], buf)
            new_conv.append(st)
        x = x + gt1[:, None, :] * mix
        h = modulate(x, sh2, sc2)
        buf = None if ffn_states is None else ffn_states[i]
        ff, st = conv_ffn(h, w_up[i], w_ffn_dw[i], w_down[i], buf)
        new_ffn.append(st)
        x = x + gt2[:, None, :] * ff
    y = rms_norm(x) * final_norm_g
    kv_out = [jnp.stack(lst) for lst in new_kv]
    return y, kv_out, jnp.stack(new_conv), jnp.stack(new_ffn)


def _normal(k, shape, scale):
    return jax.random.normal(k, shape, jnp.float32) * scale


def setup_inputs(seed: int = 0) -> dict:
    key = jax.random.key(seed)
    ks = jax.random.split(key, 32)
    D, F = D_MODEL, D_FF
    buf_lens = [min(w, PAST_LEN) for (w, _) in DILATED_GROUPS]
    return {
        'x_prompt': _normal(ks[0], (BATCH, SEQ, D), 1.0),
        'x_sample': _normal(ks[1], (DEC_BATCH, DEC_SEQ, D), 1.0),
        'c_prompt': _normal(ks[2], (BATCH, D), 1.0),
        'c_sample': _normal(ks[3], (DEC_BATCH, D), 1.0),
        'cache_kv_g1': _normal(ks[4], (N_ATTN_LAYERS, DEC_BATCH, buf_lens[0], 2, HEADS_PER_GROUP, HEAD_DIM), 1.0),
        'cache_kv_g2': _normal(ks[5], (N_ATTN_LAYERS, DEC_BATCH, buf_lens[1], 2, HEADS_PER_GROUP, HEAD_DIM), 1.0),
        'cache_kv_g3': _normal(ks[6], (N_ATTN_LAYERS, DEC_BATCH, buf_lens[2], 2, HEADS_PER_GROUP, HEAD_DIM), 1.0),
        'state_conv': _normal(ks[7], (N_CONV_LAYERS, DEC_BATCH, CONF_KERNEL - 1, D), 0.5),
        'state_ffn_conv': _normal(ks[8], (DEPTH, DEC_BATCH, FFN_KERNEL - 1, 2 * F), 1.0),
        'w_ada': _normal(ks[9], (DEPTH, D, 6 * D), D ** -0.5),
        'b_ada': _normal(ks[10], (DEPTH, 6 * D), 0.02),
        'w_qkv': _normal(ks[11], (N_ATTN_LAYERS, D, QKV_WIDTH), D ** -0.5),
        'w_o': _normal(ks[12], (N_ATTN_LAYERS, ATTN_WIDTH, D), ATTN_WIDTH ** -0.5),
        'w_pw1': _normal(ks[13], (N_CONV_LAYERS, D, 2 * D), D ** -0.5),
        'b_pw1': _normal(ks[14], (N_CONV_LAYERS, 2 * D), 0.02),
        'w_dw': _normal(ks[15], (N_CONV_LAYERS, CONF_KERNEL, D), CONF_KERNEL ** -0.5),
        'b_dw': _normal(ks[16], (N_CONV_LAYERS, D), 0.02),
        'ln_g': 1.0 + _normal(ks[17], (N_CONV_LAYERS, D), 0.02),
        'ln_b': _normal(ks[18], (N_CONV_LAYERS, D), 0.02),
        'w_pw2': _normal(ks[19], (N_CONV_LAYERS, D, D), D ** -0.5),
        'b_pw2': _normal(ks[20], (N_CONV_LAYERS, D), 0.02),
        'w_up': _normal(ks[21], (DEPTH, D, 2 * F), D ** -0.5),
        'w_ffn_dw': _normal(ks[22], (DEPTH, FFN_KERNEL, 2 * F), FFN_KERNEL ** -0.5),
        'w_down': _normal(ks[23], (DEPTH, F, D), F ** -0.5),
        'final_norm_g': 1.0 + _normal(ks[24], (D,), 0.02),
    }


def reference(x_prompt, x_sample, c_prompt, c_sample, cache_kv_g1, cache_kv_g2, cache_kv_g3,
              state_conv, state_ffn_conv, w_ada, b_ada, w_qkv, w_o, w_pw1, b_pw1, w_dw, b_dw,
              ln_g, ln_b, w_pw2, b_pw2, w_up, w_ffn_dw, w_down, final_norm_g):
    params = (w_ada, b_ada, w_qkv, w_o, w_pw1, b_pw1, w_dw, b_dw, ln_g, ln_b, w_pw2, b_pw2,
              w_up, w_ffn_dw, w_down, final_norm_g)
    y_prompt, kv_p, conv_prompt, ffn_conv_prompt = trunk(x_prompt, c_prompt, None, None, None, params)
    y_sample, kv_s, conv_sample, ffn_conv_sample = trunk(
        x_sample, c_sample, (cache_kv_g1, cache_kv_g2, cache_kv_g3), state_conv, state_ffn_conv, params)
    kv_g1_prompt, kv_g2_prompt, kv_g3_prompt = kv_p
    kv_g1_sample, kv_g2_sample, kv_g3_sample = kv_s
    return (y_prompt, y_sample, kv_g1_prompt, kv_g2_prompt, kv_g3_prompt, conv_prompt, ffn_conv_prompt,
            kv_g1_sample, kv_g2_sample, kv_g3_sample, conv_sample, ffn_conv_sample)
```

```python
import numpy as np
from contextlib import ExitStack
import concourse.bass as bass
import concourse.mybir as mybir
from concourse.bass_utils import run_bass_kernel_spmd

F32 = mybir.dt.float32
BF16 = mybir.dt.bfloat16
AF = mybir.ActivationFunctionType
ALU = mybir.AluOpType
AX = mybir.AxisListType

NCORES = 4
D = 4096
KC = 32
P = 2048
NS = 2
TS = 8
T = P + NS * TS
F = 11008
FC = 86
QKVW = 18432
WB = [128, 512, 2048]
DIL = [1, 4, 16]
WPAD = [256, 768, 2304]
RMS_EPS = 1e-6
LN_EPS = 1e-5
SCALE = 128 ** -0.5
NEG = -30000.0
DOWN_SEGW = 688
CG = [(0, 2048, 0), (2048, 2056, 1), (2056, 2064, 2)]

V_BADA = 0
V_BPW1 = 384
V_WDW = 448
V_BDW = 1440
V_LNG = 1472
V_LNB = 1504
V_BPW2 = 1536
V_WFD = 1568
V_FNG = 2600
NV = 2632


def subs(w):
    n = -(-w // 512)
    base, rem = divmod(w, n)
    out = []
    c = 0
    for j in range(n):
        cw = base + (1 if j < rem else 0)
        out.append((c, cw, j * 512))
        c += cw
    return out


def segs(total, segw):
    out = []
    c = 0
    while c < total:
        out.append((c, min(segw, total - c)))
        c += segw
    return out


class Chain:
    LIMIT = 30000
    _n = [0]

    def __init__(self, nc, stack, prevs=None):
        self.nc = nc
        self.stack = stack
        self.prevs = list(prevs) if prevs else []
        self._new()

    def _new(self):
        Chain._n[0] += 1
        self.sem = self.stack.enter_context(self.nc.semaphore(f"chain{Chain._n[0]}"))
        self.cnt = 0

    @property
    def prev(self):
        return self.prevs[-1]

    def wait_all(self, eng):
        for p in self.prevs:
            eng.wait_ge(p[0], p[1])

    def run(self, eng, fn, dma=False):
        if self.cnt > self.LIMIT:
            self._new()
        self.wait_all(eng)
        r = fn()
        lst = list(r) if isinstance(r, (list, tuple)) else [r]
        if dma:
            for ins in lst:
                ins.then_inc(self.sem, 16)
                self.cnt += 16
        else:
            lst[-1].then_inc(self.sem, 1)
            self.cnt += 1
        self.prevs = [(self.sem, self.cnt)]

    def fork(self, n=2):
        if not hasattr(self, "_lanes"):
            self._lanes = [Chain(self.nc, self.stack) for _ in range(n)]
        for c in self._lanes:
            c.prevs = list(self.prevs)
        return self._lanes

    def join(self, subs):
        self.prevs = [p for c in subs for p in c.prevs]


_UC = [0]


def U(name):
    _UC[0] += 1
    return f"{name}_{_UC[0]}"


def build_program():
    nc = bass.Bass("TRN2", target_bir_lowering=False)

    def din(name, shape, dt=F32):
        return nc.dram_tensor(name, list(shape), dt, kind="ExternalInput").ap()

    def dout(name, shape, dt=F32):
        return nc.dram_tensor(name, list(shape), dt, kind="ExternalOutput").ap()

    def dscr(name, shape, dt=F32):
        return nc.dram_tensor(name, list(shape), dt).ap()

    xT = din("xT", [D, T])
    cT = din("cT", [128, KC * 3])
    vecs = din("vecs", [128, NV])
    masks = din("masks", [128, 512])
    identf_d = din("identf", [128, 128])
    w_ada_t = din("w_ada_t", [2 * 192, 128, KC * 128])
    w_qkv_t = din("w_qkv_t", [144, 128, KC * 128])
    w_o_t = din("w_o_t", [32, 128, 16 * 128])
    w_pw1_t = din("w_pw1_t", [64, 128, KC * 128])
    w_pw2_t = din("w_pw2_t", [32, 128, KC * 128])
    w_up_t = din("w_up_t", [2 * 172, 128, KC * 128])
    w_down_t = din("w_down_t", [2 * 32, 128, FC * 128])
    ck = [din(f"ck{g}", [NS, WB[g], 4096]) for g in range(3)]
    sconvT = din("sconvT", [NS, D, 30])
    sffnT = din("sffnT", [2, NS, 2 * F, 2])
    yT = dout("yT", [D, T])
    kvpT = [dout(f"kvp{g}T", [4096, WB[g]]) for g in range(3)]
    convpT = dout("convpT", [D, 30])
    ffnpT = dout("ffnpT", [2, 2 * F, 2])
    kvs_old = [dout(f"kvs_old{g}", [NS, WB[g] - 8, 4096]) for g in range(3)]
    kvs_newT = dout("kvs_newT", [NS, 3, 4096, 8])
    convsT = dout("convsT", [NS, D, 30])
    ffnsT = dout("ffnsT", [2, NS, 2 * F, 2])
    XA = dscr("XA", [D, T])
    XB = dscr("XB", [D, T])
    H = dscr("H", [D, T], BF16)
    QKVb = dscr("QKVb", [QKVW, T], BF16)
    KVT = [dscr(f"KVT{g}", [NS, 4096, WPAD[g]], BF16) for g in range(3)]
    OATT = dscr("OATT", [T, 3, 16, 129])
    AT = dscr("AT", [2048, T], BF16)
    PW1 = dscr("PW1", [2 * D, T])
    Y = dscr("Y", [D, T])
    UP = dscr("UP", [2 * F, T])
    G = dscr("G", [F, T], BF16)

    with ExitStack() as stack:
        ch = Chain(nc, stack)
        E = stack.enter_context

        def sb(name, shape, dt=F32):
            return E(nc.sbuf_tensor(name, list(shape), dt))

        psF = E(nc.psum_tensor("psF", [128, 6 * 512], F32))
        psB = E(nc.psum_tensor("psB", [128, 2 * 1024], BF16))
        vec = sb("vec", [128, NV])
        ada = sb("ada", [128, 2 * 192 * 3])
        msk = sb("msk", [128, 512])
        identf = sb("identf_sb", [128, 128])
        identb = sb("identb_sb", [128, 128], BF16)
        onesf = sb("onesf", [128, 128])

        V = nc.vector
        S = nc.scalar
        PE = nc.tensor
        GP = nc.gpsimd
        SY = nc.sync

        def A(l, j, kc, r):
            o = ((l * 192 + j * 32 + kc) * 3 + r)
            return ada[:, o:o + 1]

        def vcol(o):
            return vec[:, o:o + 1]

        ch.run(SY, lambda: [SY.dma_start(out=vec[:], in_=vecs[:, :]),
                            SY.dma_start(out=msk[:], in_=masks[:, :]),
                            SY.dma_start(out=identf[:], in_=identf_d[:, :])], dma=True)
        ch.run(V, lambda: V.tensor_copy(out=identb[:], in_=identf[:]))
        ch.run(V, lambda: V.memset(onesf[:], 1.0))

        with ExitStack() as ph:
            c32 = ph.enter_context(nc.sbuf_tensor(U("c32"), [128, KC * 3], F32))
            scb = ph.enter_context(nc.sbuf_tensor(U("scb"), [128, KC * 3], BF16))
            was = [ph.enter_context(nc.sbuf_tensor(U("wa"), [128, KC * 128], BF16)) for _ in range(2)]
            ch.run(SY, lambda: SY.dma_start(out=c32[:], in_=cT[:, :]), dma=True)
            ch.run(S, lambda: S.activation(out=scb[:], in_=c32[:], func=AF.Silu))
            cs = ch.fork(2)
            it = 0
            for l in range(2):
                for m in range(192):
                    c = cs[it % 2]
                    wa = was[it % 2]
                    pc = (it % 2) * 512
                    it += 1
                    c.run(GP, lambda: GP.dma_start(out=wa[:], in_=w_ada_t[l * 192 + m]), dma=True)
                    c.run(PE, lambda: [PE.matmul(psF[:, pc:pc + 3], lhsT=wa[:, kc * 128:(kc + 1) * 128],
                                                 rhs=scb[:, kc * 3:(kc + 1) * 3],
                                                 start=(kc == 0), stop=(kc == KC - 1)) for kc in range(KC)])
                    o = (l * 192 + m) * 3
                    c.run(S, lambda: S.activation(out=ada[:, o:o + 3], in_=psF[:, pc:pc + 3], func=AF.Identity,
                                                  bias=vcol(V_BADA + l * 192 + m), scale=1.0))
                    if (m // 32) in (1, 4):
                        c.run(V, lambda: V.tensor_scalar_add(out=ada[:, o:o + 3], in0=ada[:, o:o + 3], scalar1=1.0))
            ch.join(cs)

        def stats_rms(xsrc, xcs, sqs, rstd):
            cs = ch.fork(2)
            for kc in range(KC):
                c, xc, sq = cs[kc % 2], xcs[kc % 2], sqs[kc % 2]
                c.run(GP, lambda: GP.dma_start(out=xc[:], in_=xsrc[kc * 128:(kc + 1) * 128, :]), dma=True)
                c.run(S, lambda: S.activation(out=sq[:], in_=xc[:], func=AF.Square))
                c.run(PE, lambda: [PE.matmul(psF[:, pc:pc + cw], lhsT=onesf[:], rhs=sq[:, c0:c0 + cw],
                                             start=(kc == 0), stop=(kc == KC - 1)) for (c0, cw, pc) in subs(T)])
            ch.join(cs)
            ch.run(S, lambda: [S.activation(out=rstd[:, c0:c0 + cw], in_=psF[:, pc:pc + cw], func=AF.Sqrt,
                                            bias=RMS_EPS, scale=1.0 / D) for (c0, cw, pc) in subs(T)])
            ch.run(V, lambda: V.reciprocal(out=rstd[:], in_=rstd[:]))

        def prep(xsrc, l, jsh, jsc):
            with ExitStack() as ph:
                xcs = [ph.enter_context(nc.sbuf_tensor(U("p_xc"), [128, T], F32)) for _ in range(2)]
                sqs = [ph.enter_context(nc.sbuf_tensor(U("p_sq"), [128, T], F32)) for _ in range(2)]
                rstd = ph.enter_context(nc.sbuf_tensor(U("p_rstd"), [128, T], F32))
                hbs = [ph.enter_context(nc.sbuf_tensor(U("p_hb"), [128, T], BF16)) for _ in range(2)]
                stats_rms(xsrc, xcs, sqs, rstd)
                cs = ch.fork(2)
                for kc in range(KC):
                    c, xc, sq, hb = cs[kc % 2], xcs[kc % 2], sqs[kc % 2], hbs[kc % 2]
                    c.run(GP, lambda: GP.dma_start(out=xc[:], in_=xsrc[kc * 128:(kc + 1) * 128, :]), dma=True)
                    c.run(V, lambda: V.tensor_tensor(out=sq[:], in0=xc[:], in1=rstd[:], op=ALU.mult))
                    c.run(S, lambda: [S.activation(out=hb[:, a:b], in_=sq[:, a:b], func=AF.Identity,
                                                   bias=A(l, jsh, kc, r), scale=A(l, jsc, kc, r))
                                      for (a, b, r) in CG])
                    c.run(SY, lambda: SY.dma_start(out=H[kc * 128:(kc + 1) * 128, :], in_=hb[:]), dma=True)
                ch.join(cs)

        def gemm_pipe(wt, m0, M, kcn, act, segw, kind, NW=3, dst=None, bias_off=None, xin=None, xout=None,
                      l=None, jg=None):
            with ExitStack() as ph:
                EP = ph.enter_context
                actb = EP(nc.sbuf_tensor(U("g_act"), [128, kcn, segw], BF16))
                wr = [EP(nc.sbuf_tensor(U("g_w"), [128, kcn * 128], BF16)) for _ in range(NW)]
                ot = [EP(nc.sbuf_tensor(U("g_ot"), [128, segw], F32)) for _ in range(2)]
                if kind == "resid":
                    xi = [EP(nc.sbuf_tensor(U("g_xi"), [128, segw], F32)) for _ in range(2)]
                    o2 = [EP(nc.sbuf_tensor(U("g_o2"), [128, segw], F32)) for _ in range(2)]
                if kind == "qkv":
                    ob = [EP(nc.sbuf_tensor(U("g_ob"), [128, segw], BF16)) for _ in range(2)]
                PD = EP(nc.semaphore(U("PD")))
                EV = EP(nc.semaphore(U("EV")))
                WF = EP(nc.semaphore(U("WF")))
                AL = EP(nc.semaphore(U("AL")))
                OD = EP(nc.semaphore(U("OD")))
                XF = EP(nc.semaphore(U("XF")))
                VD = EP(nc.semaphore(U("VD")))
                for eng in (GP, SY, PE, S, V):
                    ch.wait_all(eng)
                act3 = act.rearrange("(kc p) t -> p kc t", p=128)
                items = [(si, s0, sw, m) for si, (s0, sw) in enumerate(segs(T, segw)) for m in range(M)]
                od_after = []
                n_od = 0
                for i, (si, s0, sw, m) in enumerate(items):
                    p = i % 2
                    wbuf = wr[i % NW]
                    pb = p * 1536
                    if m == 0:
                        if si > 0:
                            SY.wait_ge(PD, i)
                        SY.dma_start(out=actb[:, :, 0:sw], in_=act3[:, :, s0:s0 + sw]).then_inc(AL, 16)
                    if i >= NW:
                        GP.wait_ge(PD, i - NW + 1)
                    GP.dma_start(out=wbuf[:], in_=wt[m0 + m]).then_inc(WF, 16)
                    if kind == "resid":
                        if i >= 2:
                            GP.wait_ge(VD, i - 1)
                        GP.dma_start(out=xi[p][:, 0:sw], in_=xin[m * 128:(m + 1) * 128, s0:s0 + sw]).then_inc(XF, 16)
                    PE.wait_ge(WF, 16 * (i + 1))
                    if m == 0:
                        PE.wait_ge(AL, 16 * (si + 1))
                    if i >= 2:
                        PE.wait_ge(EV, i - 1)
                    mm = [PE.matmul(psF[:, pb + pc:pb + pc + cw], lhsT=wbuf[:, kc * 128:(kc + 1) * 128],
                                    rhs=actb[:, kc, c0:c0 + cw], start=(kc == 0), stop=(kc == kcn - 1))
                          for (c0, cw, pc) in subs(sw) for kc in range(kcn)]
                    mm[-1].then_inc(PD, 1)
                    S.wait_ge(PD, i + 1)
                    if i >= 2:
                        if kind == "resid":
                            S.wait_ge(VD, i - 1)
                        else:
                            S.wait_ge(OD, 16 * od_after[i - 2])
                    bias = 0.0 if bias_off is None else vcol(bias_off + m)
                    ev = [S.activation(out=ot[p][:, c0:c0 + cw], in_=psF[:, pb + pc:pb + pc + cw], func=AF.Identity,
                                       bias=bias, scale=1.0) for (c0, cw, pc) in subs(sw)]
                    if kind == "qkv":
                        ev += [S.activation(out=ob[p][:, c0:c0 + cw], in_=psF[:, pb + pc:pb + pc + cw],
                                            func=AF.Identity, bias=0.0, scale=1.0) for (c0, cw, pc) in subs(sw)]
                    ev[-1].then_inc(EV, 1)
                    if kind == "store":
                        SY.wait_ge(EV, i + 1)
                        SY.dma_start(out=dst[m * 128:(m + 1) * 128, s0:s0 + sw], in_=ot[p][:, 0:sw]).then_inc(OD, 16)
                        n_od += 1
                    elif kind == "resid":
                        V.wait_ge(EV, i + 1)
                        V.wait_ge(XF, 16 * (i + 1))
                        if i >= 2:
                            V.wait_ge(OD, 16 * (i - 1))
                        ops = []
                        for (a, b, r) in CG:
                            lo, hi = max(a, s0), min(b, s0 + sw)
                            if lo < hi:
                                ops.append((lo - s0, hi - s0, r))
                        vi = [V.scalar_tensor_tensor(out=o2[p][:, a:b], in0=ot[p][:, a:b], scalar=A(l, jg, m, r),
                                                     in1=xi[p][:, a:b], op0=ALU.mult, op1=ALU.add)
                              for (a, b, r) in ops]
                        vi[-1].then_inc(VD, 1)
                        SY.wait_ge(VD, i + 1)
                        SY.dma_start(out=xout[m * 128:(m + 1) * 128, s0:s0 + sw], in_=o2[p][:, 0:sw]).then_inc(OD, 16)
                        n_od += 1
                    else:
                        g, rem = divmod(m, 48)
                        j, h = divmod(rem, 16)
                        SY.wait_ge(EV, i + 1)
                        SY.dma_start(out=QKVb[m * 128:(m + 1) * 128, s0:s0 + sw], in_=ob[p][:, 0:sw]).then_inc(OD, 16)
                        n_od += 1
                        if j > 0:
                            frow = (j - 1) * 2048 + h * 128
                            lo, hi = max(P - WB[g], s0), min(P, s0 + sw)
                            if lo < hi:
                                SY.dma_start(out=kvpT[g][frow:frow + 128, lo - (P - WB[g]):hi - (P - WB[g])],
                                             in_=ot[p][:, lo - s0:hi - s0]).then_inc(OD, 16)
                                n_od += 1
                            if s0 + sw == T:
                                for s in range(NS):
                                    c = P + TS * s - s0
                                    SY.dma_start(out=kvs_newT[s, g, frow:frow + 128, :],
                                                 in_=ot[p][:, c:c + TS]).then_inc(OD, 16)
                                    SY.dma_start(out=KVT[g][s, frow:frow + 128, WB[g]:WB[g] + TS],
                                                 in_=ob[p][:, c:c + TS]).then_inc(OD, 16)
                                    n_od += 2
                    od_after.append(n_od)
                ch.prevs = [(OD, 16 * n_od)]

        def gemm_up(l):
            wt, m0, kcn, segw, NW = w_up_t, l * 172, KC, 1032, 3
            with ExitStack() as ph:
                EP = ph.enter_context
                actb = EP(nc.sbuf_tensor(U("u_act"), [128, kcn, segw], BF16))
                wr = [EP(nc.sbuf_tensor(U("u_w"), [128, kcn * 128], BF16)) for _ in range(NW)]
                ot = [EP(nc.sbuf_tensor(U("u_ot"), [128, segw + 2], F32)) for _ in range(4)]
                cgb = [EP(nc.sbuf_tensor(U("u_cg"), [128, segw], F32)) for _ in range(2)]
                cvb = [EP(nc.sbuf_tensor(U("u_cv"), [128, segw], F32)) for _ in range(2)]
                gbb = [EP(nc.sbuf_tensor(U("u_gb"), [128, segw], BF16)) for _ in range(2)]
                carry = EP(nc.sbuf_tensor(U("u_carry"), [128, 172, 2], F32))
                sst = [[EP(nc.sbuf_tensor(U("u_ss"), [128, NS, TS + 2], F32)) for _ in range(2)] for _ in range(2)]
                PD, EV, WF, AL, OD, XF, CD, SD, VD, GD = (EP(nc.semaphore(U(n))) for n in
                                                          ("PD", "EV", "WF", "AL", "OD", "XF", "CD", "SD", "VD", "GD"))
                dc = Chain(nc, ph)
                for eng in (GP, SY, PE, S, V):
                    ch.wait_all(eng)
                dc.run(V, lambda: [V.memset(o[:, 0:2], 0.0) for o in ot])
                act3 = H.rearrange("(kc p) t -> p kc t", p=128)
                items = [(si, s0, sw, f, half) for si, (s0, sw) in enumerate(segs(T, segw))
                         for f in range(FC) for half in range(2)]
                od_after = []
                n_od = 0
                n_xf = 0
                for i, (si, s0, sw, f, half) in enumerate(items):
                    m = f + FC * half
                    q = i // 2
                    p = i % 2
                    pb = p * 1536
                    wbuf = wr[i % NW]
                    o = ot[i % 4]
                    last_seg = (s0 + sw == T)
                    npr = P - s0 if last_seg else sw
                    if f == 0 and half == 0:
                        if si > 0:
                            SY.wait_ge(PD, i)
                        SY.dma_start(out=actb[:, :, 0:sw], in_=act3[:, :, s0:s0 + sw]).then_inc(AL, 16)
                    if i >= NW:
                        GP.wait_ge(PD, i - NW + 1)
                    GP.dma_start(out=wbuf[:], in_=wt[m0 + m]).then_inc(WF, 16)
                    if last_seg and half == 0:
                        if q >= 2:
                            GP.wait_ge(CD, q - 1)
                        for hh in range(2):
                            row = (f + FC * hh) * 128
                            for s in range(NS):
                                GP.dma_start(out=sst[q % 2][hh][:, s, 0:2],
                                             in_=sffnT[l, s, row:row + 128, :]).then_inc(XF, 16)
                                n_xf += 1
                    PE.wait_ge(WF, 16 * (i + 1))
                    if f == 0 and half == 0:
                        PE.wait_ge(AL, 16 * (si + 1))
                    if i >= 2:
                        PE.wait_ge(EV, i - 1)
                    mm = [PE.matmul(psF[:, pb + pc:pb + pc + cw], lhsT=wbuf[:, kc * 128:(kc + 1) * 128],
                                    rhs=actb[:, kc, c0:c0 + cw], start=(kc == 0), stop=(kc == kcn - 1))
                          for (c0, cw, pc) in subs(sw) for kc in range(kcn)]
                    mm[-1].then_inc(PD, 1)
                    S.wait_ge(PD, i + 1)
                    if i >= 4:
                        S.wait_ge(CD, q - 1)
                        S.wait_ge(OD, 16 * od_after[i - 4])
                    ev = [S.activation(out=o[:, 2 + c0:2 + c0 + cw], in_=psF[:, pb + pc:pb + pc + cw],
                                       func=AF.Identity, bias=0.0, scale=1.0) for (c0, cw, pc) in subs(sw)]
                    ev[-1].then_inc(EV, 1)
                    if last_seg:
                        SY.wait_ge(EV, i + 1)
                        SY.dma_start(out=ffnpT[l, m * 128:(m + 1) * 128, :], in_=o[:, npr:npr + 2]).then_inc(OD, 16)
                        for s in range(NS):
                            c = 2 + npr + TS * s + TS - 2
                            SY.dma_start(out=ffnsT[l, s, m * 128:(m + 1) * 128, :], in_=o[:, c:c + 2]).then_inc(OD, 16)
                        n_od += 3
                    od_after.append(n_od)
                    if half == 0:
                        continue
                    V.wait_ge(EV, i + 1)
                    if q >= 2:
                        V.wait_ge(VD, q - 1)
                    if last_seg:
                        V.wait_ge(XF, 16 * n_xf)
                    for hh in range(2):
                        oo = ot[(i - 1 + hh) % 4]
                        mm_ = f + FC * hh
                        res = (cgb if hh == 0 else cvb)[q % 2]
                        wk_ = [vcol(V_WFD + (l * 3 + k) * 172 + mm_) for k in range(3)]
                        if not last_seg:
                            dc.run(V, lambda: V.tensor_copy(out=carry[:, mm_, :], in_=oo[:, sw:sw + 2]))
                        else:
                            dc.run(V, lambda: V.tensor_copy(out=oo[:, 0:2], in_=carry[:, mm_, :]))
                        dc.run(V, lambda: V.tensor_scalar(out=res[:, 0:sw], in0=oo[:, 0:sw], scalar1=wk_[0], scalar2=None,
                                                          op0=ALU.mult))
                        for k in (1, 2):
                            dc.run(V, lambda: V.scalar_tensor_tensor(out=res[:, 0:sw], in0=oo[:, k:sw + k], scalar=wk_[k],
                                                                     in1=res[:, 0:sw], op0=ALU.mult, op1=ALU.add))
                        if last_seg:
                            bs = sst[q % 2][hh]
                            rs = res[:, npr:npr + NS * TS].rearrange("p (s t) -> p s t", t=TS)
                            dc.run(V, lambda: V.tensor_copy(
                                out=bs[:, :, 2:TS + 2],
                                in_=oo[:, 2 + npr:2 + npr + NS * TS].rearrange("p (s t) -> p s t", t=TS)))
                            dc.run(V, lambda: V.tensor_scalar(out=rs, in0=bs[:, :, 0:TS], scalar1=wk_[0], scalar2=None,
                                                              op0=ALU.mult))
                            for k in (1, 2):
                                dc.run(V, lambda: V.scalar_tensor_tensor(out=rs, in0=bs[:, :, k:TS + k], scalar=wk_[k],
                                                                         in1=rs, op0=ALU.mult, op1=ALU.add))
                    V.wait_ge(dc.prev[0], dc.prev[1])
                    V.engine_nop().then_inc(CD, 1)
                    cg, cv, gb = cgb[q % 2], cvb[q % 2], gbb[q % 2]
                    S.wait_ge(CD, q + 1)
                    S.activation(out=cg[:, 0:sw], in_=cg[:, 0:sw], func=AF.Silu).then_inc(SD, 1)
                    V.wait_ge(SD, q + 1)
                    if q >= 2:
                        V.wait_ge(GD, 16 * (q - 1))
                    V.tensor_tensor(out=gb[:, 0:sw], in0=cg[:, 0:sw], in1=cv[:, 0:sw], op=ALU.mult).then_inc(VD, 1)
                    SY.wait_ge(VD, q + 1)
                    SY.dma_start(out=G[f * 128:(f + 1) * 128, s0:s0 + sw], in_=gb[:, 0:sw]).then_inc(GD, 16)
                npairs = len(items) // 2
                ch.prevs = [(GD, 16 * npairs), (OD, 16 * n_od)]

        def gemm_store(wt, m0, M, kcn, act, segw, dst, bias_off=None):
            gemm_pipe(wt, m0, M, kcn, act, segw, "store", dst=dst, bias_off=bias_off)

        def gemm_resid(wt, m0, M, kcn, act, segw, xin, xout, l, jg, bias_off=None):
            gemm_pipe(wt, m0, M, kcn, act, segw, "resid", NW=(2 if kcn > 32 else 3), xin=xin, xout=xout, l=l, jg=jg,
                      bias_off=bias_off)

        prep(xT, 0, 0, 1)

        for g in range(3):
            ch.run(SY, lambda: [SY.dma_start(out=kvs_old[g][s], in_=ck[g][s, 8:WB[g], :]) for s in range(NS)],
                   dma=True)
        with ExitStack() as ph:
            cins = [ph.enter_context(nc.sbuf_tensor(U("ct_in"), [128, 4096], F32)) for _ in range(2)]
            couts = [ph.enter_context(nc.sbuf_tensor(U("ct_out"), [128, 32, 128], BF16)) for _ in range(2)]
            cs = ch.fork(2)
            it = 0
            for g in range(3):
                for s in range(NS):
                    kv3 = KVT[g][s].rearrange("(c p) w -> p c w", p=128)
                    for rt in range(WB[g] // 128):
                        c, cin, cout = cs[it % 2], cins[it % 2], couts[it % 2]
                        pc = (it % 2) * 512
                        it += 1
                        c.run(GP, lambda: GP.dma_start(out=cin[:], in_=ck[g][s, rt * 128:(rt + 1) * 128, :]),
                              dma=True)
                        for q4 in range(8):
                            c.run(PE, lambda: [PE.transpose(psF[:, pc + i * 128:pc + (i + 1) * 128],
                                                            cin[:, (q4 * 4 + i) * 128:(q4 * 4 + i + 1) * 128],
                                                            identf[:]) for i in range(4)])
                            c.run(S, lambda: S.copy(out=cout[:, q4 * 4:(q4 + 1) * 4, :],
                                                    in_=psF[:, pc:pc + 512].rearrange("p (c w) -> p c w", w=128)))
                        c.run(SY, lambda: SY.dma_start(out=kv3[:, :, rt * 128:(rt + 1) * 128], in_=cout[:]),
                              dma=True)
            ch.join(cs)

        gemm_pipe(w_qkv_t, 0, 144, KC, H, 1032, "qkv")

        with ExitStack() as ph:
            EP = ph.enter_context
            qT = EP(nc.sbuf_tensor(U("a_q"), [128, T], BF16))
            kT = EP(nc.sbuf_tensor(U("a_k"), [128, T], BF16))
            vT = EP(nc.sbuf_tensor(U("a_v"), [128, T], BF16))
            ksb = [EP(nc.sbuf_tensor(U(f"a_ks{s}"), [128, WPAD[2]], BF16)) for s in range(NS)]
            vsb = [EP(nc.sbuf_tensor(U(f"a_vs{s}"), [128, WPAD[2]], BF16)) for s in range(NS)]
            Vall = EP(nc.sbuf_tensor(U("a_vall"), [128, 48, 128], BF16))

            def mkset(p):
                d_ = {}
                d_["Sm"] = EP(nc.sbuf_tensor(U("a_sm"), [128, 4, 256], F32))
                d_["Pb"] = EP(nc.sbuf_tensor(U("a_p"), [128, 4, 256], BF16))
                d_["PTs"] = EP(nc.sbuf_tensor(U("a_pt"), [128, 4, 2, 128], BF16))
                d_["OUT"] = EP(nc.sbuf_tensor(U("a_out"), [128, 4, 129], F32))
                for nm in ("mx", "ngm", "den", "lnd", "rden"):
                    d_[nm] = EP(nc.sbuf_tensor(U("a_" + nm), [128, 4], F32))
                d_["psS"] = psF[:, p * 1536:p * 1536 + 1024].rearrange("p (u w) -> p u w", w=256)
                d_["psO"] = psF[:, p * 1536 + 1024:p * 1536 + 1536].rearrange("p (u w) -> p u w", w=128)
                d_["psPT"] = psB[:, p * 1024:(p + 1) * 1024].rearrange("p (u i w) -> p u i w", i=2, w=128)
                return d_
            sets = [mkset(0), mkset(1)]

            def attn_batch(c, bs, units, Mq, nk2, g, h):
                NB = len(units)
                W = 128 + nk2
                Sm, Pb, PTs, OUT = bs["Sm"], bs["Pb"], bs["PTs"], bs["OUT"]
                mx, ngm, den, lnd, rden = bs["mx"], bs["ngm"], bs["den"], bs["lnd"], bs["rden"]
                psS, psO, psPT = bs["psS"], bs["psO"], bs["psPT"]
                c.run(PE, lambda: [ins for u, un in enumerate(units) for ins in (
                    PE.matmul(psS[:Mq, u, 0:128], lhsT=un["q"], rhs=un["k1"], start=True, stop=True),
                    PE.matmul(psS[:Mq, u, 128:W], lhsT=un["q"], rhs=un["k2"], start=True, stop=True))])
                yield
                c.run(V, lambda: [V.scalar_tensor_tensor(out=Sm[:Mq, u, 0:W], in0=psS[:Mq, u, 0:W], scalar=SCALE,
                                                         in1=un["mask"], op0=ALU.mult, op1=ALU.add)
                                  for u, un in enumerate(units)])
                yield
                c.run(V, lambda: V.tensor_reduce(out=mx[:Mq, 0:NB], in_=Sm[:Mq, 0:NB, 0:W], axis=AX.X, op=ALU.max))
                c.run(V, lambda: V.tensor_scalar_mul(out=ngm[:Mq, 0:NB], in0=mx[:Mq, 0:NB], scalar1=-1.0))
                c.run(V, lambda: V.memset(den[:Mq, 0:NB], 0.0))
                yield
                for u in range(NB):
                    c.run(S, lambda: S.activation(out=Pb[:Mq, u, 0:W], in_=Sm[:Mq, u, 0:W], func=AF.Exp,
                                                  bias=ngm[:Mq, u:u + 1], scale=1.0, accum_out=den[:Mq, u:u + 1]))
                yield
                c.run(PE, lambda: [ins for u in range(NB) for ins in (
                    PE.transpose(psPT[:128, u, 0, 0:Mq], Pb[:Mq, u, 0:128], identb[:Mq, :Mq]),
                    PE.transpose(psPT[:nk2, u, 1, 0:Mq], Pb[:Mq, u, 128:W], identb[:Mq, :Mq]))])
                yield
                c.run(S, lambda: [S.copy(out=PTs[:128, 0:NB, 0, 0:Mq], in_=psPT[:128, 0:NB, 0, 0:Mq]),
                                  S.copy(out=PTs[:nk2, 0:NB, 1, 0:Mq], in_=psPT[:nk2, 0:NB, 1, 0:Mq])])
                yield
                c.run(PE, lambda: [ins for u, un in enumerate(units) for ins in (
                    PE.matmul(psO[:Mq, u, :], lhsT=PTs[:128, u, 0, 0:Mq], rhs=Vall[:128, un["t1"], :],
                              start=True, stop=False),
                    PE.matmul(psO[:Mq, u, :], lhsT=PTs[:nk2, u, 1, 0:Mq], rhs=Vall[:nk2, un["t2"], :],
                              start=False, stop=True))])
                yield
                c.run(S, lambda: S.activation(out=lnd[:Mq, 0:NB], in_=den[:Mq, 0:NB], func=AF.Ln))
                c.run(V, lambda: V.tensor_tensor(out=OUT[:Mq, 0:NB, 128], in0=mx[:Mq, 0:NB], in1=lnd[:Mq, 0:NB],
                                                 op=ALU.add))
                c.run(V, lambda: V.reciprocal(out=rden[:Mq, 0:NB], in_=den[:Mq, 0:NB]))
                c.run(V, lambda: [V.tensor_scalar(out=OUT[:Mq, u, 0:128], in0=psO[:Mq, u, :],
                                                  scalar1=rden[:Mq, u:u + 1], scalar2=None, op0=ALU.mult)
                                  for u in range(NB)])
                yield
                c.run(SY, lambda: [SY.dma_start(out=OATT[un["rows"], g, h, :], in_=OUT[:Mq, u, :])
                                   for u, un in enumerate(units)], dma=True)
                yield

            def lane(c, bs, batches):
                for (units, Mq, nk2, g, h) in batches:
                    yield from attn_batch(c, bs, units, Mq, nk2, g, h)

            master = msk[:, 0:256]
            maskF = msk[:, 256:512]
            for g in range(3):
                d = DIL[g]
                nb = (P // d) // 128
                for h in range(16):
                    mq, mk, mv = g * 48 + h, g * 48 + 16 + h, g * 48 + 32 + h
                    wk = WB[g] + TS
                    ch.run(GP, lambda: [GP.dma_start(out=qT[:], in_=QKVb[mq * 128:(mq + 1) * 128, :]),
                                        GP.dma_start(out=kT[:], in_=QKVb[mk * 128:(mk + 1) * 128, :]),
                                        GP.dma_start(out=vT[:], in_=QKVb[mv * 128:(mv + 1) * 128, :])] +
                           [ins for s in range(NS) for ins in (
                               GP.dma_start(out=ksb[s][:, 0:wk], in_=KVT[g][s, h * 128:(h + 1) * 128, 0:wk]),
                               GP.dma_start(out=vsb[s][:, 0:wk],
                                            in_=KVT[g][s, 2048 + h * 128:2048 + (h + 1) * 128, 0:wk]))], dma=True)
                    vtiles = []

                    def vt(ap, nk):
                        vtiles.append((ap, nk))
                        return len(vtiles) - 1
                    batches = []
                    units = []
                    blk = {}
                    for r in range(d):
                        for j in range(nb):
                            st = r + d * 128 * j
                            blk[(r, j)] = (slice(st, st + d * 127 + 1, d), vt(vT[:, st:st + d * 127 + 1:d], 128))
                    for r in range(d):
                        for j in range(nb):
                            cur, tcur = blk[(r, j)]
                            prv, tprv = blk[(r, j - 1)] if j > 0 else blk[(r, j)]
                            units.append(dict(q=qT[:, cur], k1=kT[:, prv], t1=tprv, k2=kT[:, cur], t2=tcur,
                                              mask=(master if j > 0 else maskF), rows=cur))
                    for b0 in range(0, len(units), 4):
                        batches.append((units[b0:b0 + 4], 128, 128, g, h))
                    units = []
                    if g == 0:
                        Mq, nk2 = 8, 8
                        for s in range(NS):
                            q0 = P + TS * s
                            units.append(dict(q=qT[:, q0:q0 + 8], k1=ksb[s][:, 0:128], t1=vt(vsb[s][:, 0:128], 128),
                                              k2=ksb[s][:, 128:136], t2=vt(vsb[s][:, 128:136], 8),
                                              mask=master[:8, 0:136], rows=slice(q0, q0 + 8)))
                    elif g == 1:
                        Mq, nk2 = 2, 2
                        for s in range(NS):
                            for r in range(4):
                                q0 = P + TS * s + r
                                units.append(dict(q=qT[:, q0:q0 + 5:4], k1=ksb[s][:, r:r + 509:4],
                                                  t1=vt(vsb[s][:, r:r + 509:4], 128),
                                                  k2=ksb[s][:, 512 + r:512 + r + 5:4],
                                                  t2=vt(vsb[s][:, 512 + r:512 + r + 5:4], 2),
                                                  mask=master[:2, 0:130], rows=slice(q0, q0 + 5, 4)))
                    else:
                        Mq, nk2 = 1, 1
                        for s in range(NS):
                            for r in range(8):
                                q0 = P + TS * s + r
                                units.append(dict(q=qT[:, q0:q0 + 1], k1=ksb[s][:, r:r + 2033:16],
                                                  t1=vt(vsb[s][:, r:r + 2033:16], 128),
                                                  k2=ksb[s][:, 2048 + r:2048 + r + 1],
                                                  t2=vt(vsb[s][:, 2048 + r:2048 + r + 1], 1),
                                                  mask=master[:1, 0:129], rows=slice(q0, q0 + 1)))
                    for b0 in range(0, len(units), 4):
                        batches.append((units[b0:b0 + 4], Mq, nk2, g, h))
                    assert len(vtiles) <= 48
                    for r0 in range(0, len(vtiles), 16):
                        rnd = vtiles[r0:r0 + 16]
                        ch.run(PE, lambda: [PE.transpose(psB[:nk, i * 128:(i + 1) * 128], ap, identb[:])
                                            for i, (ap, nk) in enumerate(rnd)])
                        ch.run(V, lambda: [V.tensor_copy(
                            out=Vall[:, r0 + b8:r0 + min(b8 + 8, len(rnd)), :],
                            in_=psB[:, b8 * 128:min(b8 + 8, len(rnd)) * 128].rearrange("p (t w) -> p t w", w=128))
                            for b8 in range(0, len(rnd), 8)])
                    cs = ch.fork(2)
                    lanes = [lane(cs[0], sets[0], batches[0::2]), lane(cs[1], sets[1], batches[1::2])]
                    while lanes:
                        for ln in list(lanes):
                            try:
                                next(ln)
                            except StopIteration:
                                lanes.remove(ln)
                    ch.join(cs)

        with ExitStack() as ph:
            EP = ph.enter_context
            oas = [EP(nc.sbuf_tensor(U("c_oa"), [128, 3, 16, 129], F32)) for _ in range(2)]
            lms = [EP(nc.sbuf_tensor(U("c_lm"), [128, 16], F32)) for _ in range(2)]
            exs = [EP(nc.sbuf_tensor(U("c_ex"), [128, 3, 16], F32)) for _ in range(2)]
            sms = [EP(nc.sbuf_tensor(U("c_sm"), [128, 16], F32)) for _ in range(2)]
            ocs = [EP(nc.sbuf_tensor(U("c_oc"), [128, 16, 128], F32)) for _ in range(2)]
            obs = [EP(nc.sbuf_tensor(U("c_ob"), [128, 16, 128], BF16)) for _ in range(2)]
            at3 = AT.rearrange("(c p) t -> p c t", p=128)
            cs = ch.fork(2)
            for it, (t0, np_) in enumerate(segs(T, 128)):
                c = cs[it % 2]
                oa, lm, ex, sm, oc, ob = (x[it % 2] for x in (oas, lms, exs, sms, ocs, obs))
                pc = (it % 2) * 512
                c.run(GP, lambda: GP.dma_start(out=oa[:np_], in_=OATT[t0:t0 + np_]), dma=True)
                c.run(V, lambda: V.tensor_tensor(out=lm[:np_], in0=oa[:np_, 0, :, 128], in1=oa[:np_, 1, :, 128], op=ALU.max))
                c.run(V, lambda: V.tensor_tensor(out=lm[:np_], in0=lm[:np_], in1=oa[:np_, 2, :, 128], op=ALU.max))
                for g in range(3):
                    c.run(V, lambda: V.tensor_tensor(out=ex[:np_, g, :], in0=oa[:np_, g, :, 128], in1=lm[:np_],
                                                     op=ALU.subtract))
                c.run(S, lambda: S.activation(out=ex[:np_], in_=ex[:np_], func=AF.Exp))
                c.run(V, lambda: V.tensor_tensor(out=sm[:np_], in0=ex[:np_, 0, :], in1=ex[:np_, 1, :], op=ALU.add))
                c.run(V, lambda: V.tensor_tensor(out=sm[:np_], in0=sm[:np_], in1=ex[:np_, 2, :], op=ALU.add))
                c.run(V, lambda: V.reciprocal(out=sm[:np_], in_=sm[:np_]))
                for g in range(3):
                    c.run(V, lambda: V.tensor_tensor(out=ex[:np_, g, :], in0=ex[:np_, g, :], in1=sm[:np_], op=ALU.mult))
                for hh in range(16):
                    c.run(V, lambda: V.tensor_scalar(out=oc[:np_, hh, :], in0=oa[:np_, 0, hh, 0:128],
                                                     scalar1=ex[:np_, 0, hh:hh + 1], scalar2=None, op0=ALU.mult))
                    for g in (1, 2):
                        c.run(V, lambda: V.scalar_tensor_tensor(out=oc[:np_, hh, :], in0=oa[:np_, g, hh, 0:128],
                                                                scalar=ex[:np_, g, hh:hh + 1], in1=oc[:np_, hh, :],
                                                                op0=ALU.mult, op1=ALU.add))
                for q4 in range(4):
                    c.run(PE, lambda: [PE.transpose(psF[:, pc + i * 128:pc + i * 128 + np_], oc[:np_, q4 * 4 + i, :],
                                                    identf[:np_, :np_]) for i in range(4)])
                    c.run(S, lambda: S.copy(out=ob[:, q4 * 4:(q4 + 1) * 4, 0:np_],
                                            in_=psF[:, pc:pc + 512].rearrange("p (c w) -> p c w", w=128)[:, :, 0:np_]))
                c.run(SY, lambda: SY.dma_start(out=at3[:, :, t0:t0 + np_], in_=ob[:, :, 0:np_]), dma=True)
            ch.join(cs)

        gemm_resid(w_o_t, 0, 32, 16, AT, 1032, xT, XA, 0, 2)

        def ffn(l, xin, xout):
            prep(xin, l, 3, 4)
            gemm_up(l)
            gemm_resid(w_down_t, l * 32, 32, FC, G, DOWN_SEGW, xin, xout, l, 5)

        ffn(0, XA, XB)

        prep(XB, 1, 0, 1)
        gemm_store(w_pw1_t, 0, 64, KC, H, 1032, PW1, bias_off=V_BPW1)
        with ExitStack() as ph:
            EP = ph.enter_context

            def two(name, shape, dt=F32):
                return [EP(nc.sbuf_tensor(U(name), shape, dt)) for _ in range(2)]
            a_s, g_s = two("v_a", [128, T]), two("v_g", [128, T])
            ups, uss = two("v_up", [128, P + 30]), two("v_us", [128, NS, TS + 30])
            yts = two("v_y", [128, T])
            mean = EP(nc.sbuf_tensor(U("v_mean"), [128, T], F32))
            rstd = EP(nc.sbuf_tensor(U("v_rstd"), [128, T], F32))
            hbs = two("v_hb", [128, T], BF16)
            ch.run(V, lambda: [V.memset(x[:, 0:30], 0.0) for x in ups])
            cs = ch.fork(2)
            for kc in range(KC):
                c = cs[kc % 2]
                a_, g_, up, us, yt = (x[kc % 2] for x in (a_s, g_s, ups, uss, yts))
                ys = yt[:, P:T].rearrange("p (s t) -> p s t", t=TS)
                c.run(GP, lambda: [GP.dma_start(out=a_[:], in_=PW1[kc * 128:(kc + 1) * 128, :]),
                                   GP.dma_start(out=g_[:], in_=PW1[D + kc * 128:D + (kc + 1) * 128, :]),
                                   GP.dma_start(out=us[:, 0, 0:30], in_=sconvT[0, kc * 128:(kc + 1) * 128, :]),
                                   GP.dma_start(out=us[:, 1, 0:30], in_=sconvT[1, kc * 128:(kc + 1) * 128, :])],
                      dma=True)
                c.run(S, lambda: S.activation(out=g_[:], in_=g_[:], func=AF.Sigmoid))
                c.run(V, lambda: [V.tensor_tensor(out=up[:, 30:P + 30], in0=a_[:, 0:P], in1=g_[:, 0:P], op=ALU.mult),
                                  V.tensor_tensor(out=us[:, :, 30:TS + 30],
                                                  in0=a_[:, P:T].rearrange("p (s t) -> p s t", t=TS),
                                                  in1=g_[:, P:T].rearrange("p (s t) -> p s t", t=TS), op=ALU.mult)])
                c.run(SY, lambda: [SY.dma_start(out=convpT[kc * 128:(kc + 1) * 128, :], in_=up[:, P:P + 30]),
                                   SY.dma_start(out=convsT[0, kc * 128:(kc + 1) * 128, :], in_=us[:, 0, TS:TS + 30]),
                                   SY.dma_start(out=convsT[1, kc * 128:(kc + 1) * 128, :], in_=us[:, 1, TS:TS + 30])],
                      dma=True)
                w0 = vcol(V_WDW + kc * 31)
                c.run(V, lambda: [V.tensor_scalar(out=yt[:, 0:P], in0=up[:, 0:P], scalar1=w0, scalar2=vcol(V_BDW + kc),
                                                  op0=ALU.mult, op1=ALU.add),
                                  V.tensor_scalar(out=ys, in0=us[:, :, 0:TS], scalar1=w0, scalar2=vcol(V_BDW + kc),
                                                  op0=ALU.mult, op1=ALU.add)])
                for j in range(1, 31):
                    wj = vcol(V_WDW + kc * 31 + j)
                    c.run(V, lambda: [V.scalar_tensor_tensor(out=yt[:, 0:P], in0=up[:, j:P + j], scalar=wj,
                                                             in1=yt[:, 0:P], op0=ALU.mult, op1=ALU.add),
                                      V.scalar_tensor_tensor(out=ys, in0=us[:, :, j:TS + j], scalar=wj, in1=ys,
                                                             op0=ALU.mult, op1=ALU.add)])
                c.run(PE, lambda: [PE.matmul(psF[:, pc:pc + cw], lhsT=onesf[:], rhs=yt[:, c0:c0 + cw],
                                             start=(kc == 0), stop=(kc == KC - 1)) for (c0, cw, pc) in subs(T)])
                c.run(SY, lambda: SY.dma_start(out=Y[kc * 128:(kc + 1) * 128, :], in_=yt[:]), dma=True)
            ch.join(cs)
            ch.run(S, lambda: [S.activation(out=mean[:, c0:c0 + cw], in_=psF[:, pc:pc + cw], func=AF.Identity,
                                            bias=0.0, scale=1.0 / D) for (c0, cw, pc) in subs(T)])
            cs = ch.fork(2)
            for kc in range(KC):
                c, yt, a_ = cs[kc % 2], yts[kc % 2], a_s[kc % 2]
                c.run(GP, lambda: GP.dma_start(out=yt[:], in_=Y[kc * 128:(kc + 1) * 128, :]), dma=True)
                c.run(V, lambda: V.tensor_tensor(out=yt[:], in0=yt[:], in1=mean[:], op=ALU.subtract))
                c.run(S, lambda: S.activation(out=a_[:], in_=yt[:], func=AF.Square))
                c.run(PE, lambda: [PE.matmul(psF[:, pc:pc + cw], lhsT=onesf[:], rhs=a_[:, c0:c0 + cw],
                                             start=(kc == 0), stop=(kc == KC - 1)) for (c0, cw, pc) in subs(T)])
            ch.join(cs)
            ch.run(S, lambda: [S.activation(out=rstd[:, c0:c0 + cw], in_=psF[:, pc:pc + cw], func=AF.Sqrt,
                                            bias=LN_EPS, scale=1.0 / D) for (c0, cw, pc) in subs(T)])
            ch.run(V, lambda: V.reciprocal(out=rstd[:], in_=rstd[:]))
            cs = ch.fork(2)
            for kc in range(KC):
                c, yt, hb = cs[kc % 2], yts[kc % 2], hbs[kc % 2]
                c.run(GP, lambda: GP.dma_start(out=yt[:], in_=Y[kc * 128:(kc + 1) * 128, :]), dma=True)
                c.run(V, lambda: V.tensor_tensor(out=yt[:], in0=yt[:], in1=mean[:], op=ALU.subtract))
                c.run(V, lambda: V.tensor_tensor(out=yt[:], in0=yt[:], in1=rstd[:], op=ALU.mult))
                c.run(S, lambda: S.activation(out=hb[:], in_=yt[:], func=AF.Silu, bias=vcol(V_LNB + kc),
                                              scale=vcol(V_LNG + kc)))
                c.run(SY, lambda: SY.dma_start(out=H[kc * 128:(kc + 1) * 128, :], in_=hb[:]), dma=True)
            ch.join(cs)
        gemm_resid(w_pw2_t, 0, 32, KC, H, 1032, XB, XA, 1, 2, bias_off=V_BPW2)

        ffn(1, XA, XB)

        with ExitStack() as ph:
            EP = ph.enter_context
            xcs = [EP(nc.sbuf_tensor(U("n_xc"), [128, T], F32)) for _ in range(2)]
            sqs = [EP(nc.sbuf_tensor(U("n_sq"), [128, T], F32)) for _ in range(2)]
            rstd = EP(nc.sbuf_tensor(U("n_rstd"), [128, T], F32))
            stats_rms(XB, xcs, sqs, rstd)
            cs = ch.fork(2)
            for kc in range(KC):
                c, xc, sq = cs[kc % 2], xcs[kc % 2], sqs[kc % 2]
                c.run(GP, lambda: GP.dma_start(out=xc[:], in_=XB[kc * 128:(kc + 1) * 128, :]), dma=True)
                c.run(V, lambda: V.tensor_tensor(out=sq[:], in0=xc[:], in1=rstd[:], op=ALU.mult))
                c.run(V, lambda: V.tensor_scalar(out=sq[:], in0=sq[:], scalar1=vcol(V_FNG + kc), scalar2=None,
                                                 op0=ALU.mult))
                c.run(SY, lambda: SY.dma_start(out=yT[kc * 128:(kc + 1) * 128, :], in_=sq[:]), dma=True)
            ch.join(cs)
        ch.wait_all(SY)
    return nc


def _tile_w(w, mt=128):
    K, N = w.shape
    kc, m = K // 128, N // 128
    return np.ascontiguousarray(w.reshape(kc, 128, m, 128).transpose(2, 1, 0, 3)).reshape(m, 128, kc * 128)


def _fm(v):
    return np.ascontiguousarray(v.reshape(-1, 128).T)


def _host_shared(inp):
    sh = {}
    sh["w_ada_t"] = np.concatenate([_tile_w(inp["w_ada"][l]) for l in range(2)], axis=0)
    sh["w_qkv_t"] = _tile_w(inp["w_qkv"][0])
    sh["w_o_t"] = _tile_w(inp["w_o"][0])
    sh["w_pw1_t"] = _tile_w(inp["w_pw1"][0])
    sh["w_pw2_t"] = _tile_w(inp["w_pw2"][0])
    sh["w_up_t"] = np.concatenate([_tile_w(inp["w_up"][l]) for l in range(2)], axis=0)
    sh["w_down_t"] = np.concatenate([_tile_w(inp["w_down"][l]) for l in range(2)], axis=0)
    vec = np.zeros((128, NV), np.float32)
    for l in range(2):
        vec[:, V_BADA + l * 192:V_BADA + (l + 1) * 192] = _fm(inp["b_ada"][l])
    vec[:, V_BPW1:V_BPW1 + 64] = _fm(inp["b_pw1"][0])
    wdw = inp["w_dw"][0]
    vec[:, V_WDW:V_WDW + 992] = wdw.reshape(31, 32, 128).transpose(2, 1, 0).reshape(128, 992)
    vec[:, V_BDW:V_BDW + 32] = _fm(inp["b_dw"][0])
    vec[:, V_LNG:V_LNG + 32] = _fm(inp["ln_g"][0])
    vec[:, V_LNB:V_LNB + 32] = _fm(inp["ln_b"][0])
    vec[:, V_BPW2:V_BPW2 + 32] = _fm(inp["b_pw2"][0])
    for l in range(2):
        for k in range(3):
            o = V_WFD + (l * 3 + k) * 172
            vec[:, o:o + 172] = _fm(inp["w_ffn_dw"][l, k])
    vec[:, V_FNG:V_FNG + 32] = _fm(inp["final_norm_g"])
    sh["vecs"] = vec
    qi = np.arange(128)[:, None]
    kj = np.arange(256)[None, :]
    valid = (kj >= qi) & (kj <= qi + 128)
    master = np.where(valid, 0.0, NEG).astype(np.float32)
    maskF = master.copy()
    maskF[:, :128] = NEG
    sh["masks"] = np.concatenate([master, maskF], axis=1)
    sh["identf"] = np.eye(128, dtype=np.float32)
    return sh


def _host_core(inp, sh, b):
    m = dict(sh)
    xs = [inp["x_prompt"][b].T] + [inp["x_sample"][2 * b + s].T for s in range(NS)]
    m["xT"] = np.ascontiguousarray(np.concatenate(xs, axis=1))
    cs = np.stack([inp["c_prompt"][b]] + [inp["c_sample"][2 * b + s] for s in range(NS)], axis=0)
    m["cT"] = np.ascontiguousarray(cs.reshape(3, KC, 128).transpose(2, 1, 0)).reshape(128, KC * 3)
    for g, name in enumerate(("cache_kv_g1", "cache_kv_g2", "cache_kv_g3")):
        m[f"ck{g}"] = np.ascontiguousarray(inp[name][0, 2 * b:2 * b + 2].reshape(NS, WB[g], 4096))
    m["sconvT"] = np.ascontiguousarray(inp["state_conv"][0, 2 * b:2 * b + 2].transpose(0, 2, 1))
    m["sffnT"] = np.ascontiguousarray(inp["state_ffn_conv"][:, 2 * b:2 * b + 2].transpose(0, 1, 3, 2))
    return m


def _assemble(results, ncores=NCORES):
    y_p = np.zeros((4, P, D), np.float32)
    y_s = np.zeros((8, TS, D), np.float32)
    kvp = [np.zeros((1, 4, WB[g], 2, 16, 128), np.float32) for g in range(3)]
    convp = np.zeros((1, 4, 30, D), np.float32)
    ffnp = np.zeros((2, 4, 2, 2 * F), np.float32)
    kvs = [np.zeros((1, 8, WB[g], 2, 16, 128), np.float32) for g in range(3)]
    convs = np.zeros((1, 8, 30, D), np.float32)
    ffns = np.zeros((2, 8, 2, 2 * F), np.float32)
    for b in range(ncores):
        r = results[b]
        y_p[b] = r["yT"][:, :P].T
        for s in range(NS):
            y_s[2 * b + s] = r["yT"][:, P + TS * s:P + TS * (s + 1)].T
        for g in range(3):
            kvp[g][0, b] = r[f"kvp{g}T"].T.reshape(WB[g], 2, 16, 128)
            for s in range(NS):
                full = np.concatenate([r[f"kvs_old{g}"][s], r["kvs_newT"][s, g].T], axis=0)
                kvs[g][0, 2 * b + s] = full.reshape(WB[g], 2, 16, 128)
        convp[0, b] = r["convpT"].T
        for l in range(2):
            ffnp[l, b] = r["ffnpT"][l].T
            for s in range(NS):
                ffns[l, 2 * b + s] = r["ffnsT"][l, s].T
        for s in range(NS):
            convs[0, 2 * b + s] = r["convsT"][s].T
    return (y_p, y_s, kvp[0], kvp[1], kvp[2], convp, ffnp, kvs[0], kvs[1], kvs[2], convs, ffns)


def kernel(**inputs):
    inp = {k: np.asarray(v) for k, v in inputs.items()}
    sh = _host_shared(inp)
    in_maps = [_host_core(inp, sh, b) for b in range(NCORES)]
    nc = build_program()
    res = run_bass_kernel_spmd(nc, in_maps, core_ids=list(range(NCORES)))
    return _assemble(res.results)
```

```python
import numpy as np
from contextlib import ExitStack
import concourse.bass as bass
import concourse.mybir as mybir
from concourse.bass_utils import run_bass_kernel_spmd

F32 = mybir.dt.float32
BF16 = mybir.dt.bfloat16
AF = mybir.ActivationFunctionType
ALU = mybir.AluOpType
AX = mybir.AxisListType

NCORES = 4
D = 4096
KC = 32
P = 2048
NS = 2
TS = 8
T = P + NS * TS
F = 11008
FC = 86
QKVW = 18432
WB = [128, 512, 2048]
DIL = [1, 4, 16]
WPAD = [256, 768, 2304]
RMS_EPS = 1e-6
LN_EPS = 1e-5
SCALE = 128 ** -0.5
NEG = -30000.0
DOWN_SEGW = 688
NL = 4
CG = [(0, 2048, 0), (2048, 2056, 1), (2056, 2064, 2)]

V_BADA = 0
V_BPW1 = 384
V_WDW = 448
V_BDW = 1440
V_LNG = 1472
V_LNB = 1504
V_BPW2 = 1536
V_WFD = 1568
V_FNG = 2600
NV = 2632


def subs(w):
    n = -(-w // 512)
    base, rem = divmod(w, n)
    out = []
    c = 0
    for j in range(n):
        cw = base + (1 if j < rem else 0)
        out.append((c, cw, j * 512))
        c += cw
    return out


def segs(total, segw):
    out = []
    c = 0
    while c < total:
        out.append((c, min(segw, total - c)))
        c += segw
    return out


class Chain:
    LIMIT = 30000
    _n = [0]

    def __init__(self, nc, stack, prevs=None):
        self.nc = nc
        self.stack = stack
        self.prevs = list(prevs) if prevs else []
        self._new()

    def _new(self):
        Chain._n[0] += 1
        self.sem = self.stack.enter_context(self.nc.semaphore(f"chain{Chain._n[0]}"))
        self.cnt = 0

    @property
    def prev(self):
        return self.prevs[-1]

    def wait_all(self, eng):
        for p in self.prevs:
            eng.wait_ge(p[0], p[1])

    def run(self, eng, fn, dma=False):
        if self.cnt > self.LIMIT:
            self._new()
        self.wait_all(eng)
        r = fn()
        lst = list(r) if isinstance(r, (list, tuple)) else [r]
        if dma:
            for ins in lst:
                ins.then_inc(self.sem, 16)
                self.cnt += 16
        else:
            lst[-1].then_inc(self.sem, 1)
            self.cnt += 1
        self.prevs = [(self.sem, self.cnt)]

    def fork(self, n=2):
        if not hasattr(self, "_lanes"):
            self._lanes = []
        while len(self._lanes) < n:
            self._lanes.append(Chain(self.nc, self.stack))
        for c in self._lanes[:n]:
            c.prevs = list(self.prevs)
        return self._lanes[:n]

    def join(self, subs):
        self.prevs = [p for c in subs for p in c.prevs]


_UC = [0]


def U(name):
    _UC[0] += 1
    return f"{name}_{_UC[0]}"


def build_program():
    nc = bass.Bass("TRN2", target_bir_lowering=False)

    def din(name, shape, dt=F32):
        return nc.dram_tensor(name, list(shape), dt, kind="ExternalInput").ap()

    def dout(name, shape, dt=F32):
        return nc.dram_tensor(name, list(shape), dt, kind="ExternalOutput").ap()

    def dscr(name, shape, dt=F32):
        return nc.dram_tensor(name, list(shape), dt).ap()

    xT = din("xT", [D, T])
    cT = din("cT", [128, KC * 3])
    vecs = din("vecs", [128, NV])
    masks = din("masks", [128, 512])
    identf_d = din("identf", [128, 128])
    w_ada_t = din("w_ada_t", [2 * 192, 128, KC * 128])
    w_qkv_t = din("w_qkv_t", [144, 128, KC * 128])
    w_o_t = din("w_o_t", [32, 128, 16 * 128])
    w_pw1_t = din("w_pw1_t", [64, 128, KC * 128])
    w_pw2_t = din("w_pw2_t", [32, 128, KC * 128])
    w_up_t = din("w_up_t", [2 * 172, 128, KC * 128])
    w_down_t = din("w_down_t", [2 * 32, 128, FC * 128])
    ck = [din(f"ck{g}", [NS, WB[g], 4096]) for g in range(3)]
    sconvT = din("sconvT", [NS, D, 30])
    sffnT = din("sffnT", [2, NS, 2 * F, 2])
    yT = dout("yT", [D, T])
    kvpT = [dout(f"kvp{g}T", [4096, WB[g]]) for g in range(3)]
    convpT = dout("convpT", [D, 30])
    ffnpT = dout("ffnpT", [2, 2 * F, 2])
    kvs_old = [dout(f"kvs_old{g}", [NS, WB[g] - 8, 4096]) for g in range(3)]
    kvs_newT = dout("kvs_newT", [NS, 3, 4096, 8])
    convsT = dout("convsT", [NS, D, 30])
    ffnsT = dout("ffnsT", [2, NS, 2 * F, 2])
    XA = dscr("XA", [D, T])
    XB = dscr("XB", [D, T])
    H = dscr("H", [D, T], BF16)
    QKVb = dscr("QKVb", [QKVW, T], BF16)
    KVT = [dscr(f"KVT{g}", [NS, 4096, WPAD[g]], BF16) for g in range(3)]
    OATT = dscr("OATT", [T, 3, 16, 129])
    AT = dscr("AT", [2048, T], BF16)
    PW1 = dscr("PW1", [2 * D, T])
    Y = dscr("Y", [D, T])
    UP = dscr("UP", [2 * F, T])
    G = dscr("G", [F, T], BF16)

    with ExitStack() as stack:
        ch = Chain(nc, stack)
        E = stack.enter_context

        def sb(name, shape, dt=F32):
            return E(nc.sbuf_tensor(name, list(shape), dt))

        psF = E(nc.psum_tensor("psF", [128, 6 * 512], F32))
        psB = E(nc.psum_tensor("psB", [128, 2 * 1024], BF16))
        vec = sb("vec", [128, NV])
        ada = sb("ada", [128, 2 * 192 * 3])
        msk = sb("msk", [128, 512])
        identf = sb("identf_sb", [128, 128])
        identb = sb("identb_sb", [128, 128], BF16)
        onesf = sb("onesf", [128, 128])

        V = nc.vector
        S = nc.scalar
        PE = nc.tensor
        GP = nc.gpsimd
        SY = nc.sync

        def A(l, j, kc, r):
            o = ((l * 192 + j * 32 + kc) * 3 + r)
            return ada[:, o:o + 1]

        def vcol(o):
            return vec[:, o:o + 1]

        ch.run(SY, lambda: [SY.dma_start(out=vec[:], in_=vecs[:, :]),
                            SY.dma_start(out=msk[:], in_=masks[:, :]),
                            SY.dma_start(out=identf[:], in_=identf_d[:, :])], dma=True)
        ch.run(V, lambda: V.tensor_copy(out=identb[:], in_=identf[:]))
        ch.run(V, lambda: V.memset(onesf[:], 1.0))

        with ExitStack() as ph:
            c32 = ph.enter_context(nc.sbuf_tensor(U("c32"), [128, KC * 3], F32))
            scb = ph.enter_context(nc.sbuf_tensor(U("scb"), [128, KC * 3], BF16))
            was = [ph.enter_context(nc.sbuf_tensor(U("wa"), [128, KC * 128], BF16)) for _ in range(4)]
            ch.run(SY, lambda: SY.dma_start(out=c32[:], in_=cT[:, :]), dma=True)
            ch.run(S, lambda: S.activation(out=scb[:], in_=c32[:], func=AF.Silu))
            cs = ch.fork(4)
            it = 0
            for l in range(2):
                for m in range(192):
                    c = cs[it % 4]
                    wa = was[it % 4]
                    pc = (it % 4) * 512
                    it += 1
                    c.run(GP, lambda: GP.dma_start(out=wa[:], in_=w_ada_t[l * 192 + m]), dma=True)
                    c.run(PE, lambda: [PE.matmul(psF[:, pc:pc + 3], lhsT=wa[:, kc * 128:(kc + 1) * 128],
                                                 rhs=scb[:, kc * 3:(kc + 1) * 3],
                                                 start=(kc == 0), stop=(kc == KC - 1)) for kc in range(KC)])
                    o = (l * 192 + m) * 3
                    c.run(S, lambda: S.activation(out=ada[:, o:o + 3], in_=psF[:, pc:pc + 3], func=AF.Identity,
                                                  bias=vcol(V_BADA + l * 192 + m), scale=1.0))
                    if (m // 32) in (1, 4):
                        c.run(V, lambda: V.tensor_scalar_add(out=ada[:, o:o + 3], in0=ada[:, o:o + 3], scalar1=1.0))
            ch.join(cs)

        def stats_rms(xsrc, xcs, sqs, rstd):
            cs = ch.fork(NL)
            for kc in range(KC):
                c, xc, sq = cs[kc % NL], xcs[kc % NL], sqs[kc % NL]
                c.run(GP, lambda: GP.dma_start(out=xc[:], in_=xsrc[kc * 128:(kc + 1) * 128, :]), dma=True)
                c.run(S, lambda: S.activation(out=sq[:], in_=xc[:], func=AF.Square))
                c.run(PE, lambda: [PE.matmul(psF[:, pc:pc + cw], lhsT=onesf[:], rhs=sq[:, c0:c0 + cw],
                                             start=(kc == 0), stop=(kc == KC - 1)) for (c0, cw, pc) in subs(T)])
            ch.join(cs)
            ch.run(S, lambda: [S.activation(out=rstd[:, c0:c0 + cw], in_=psF[:, pc:pc + cw], func=AF.Sqrt,
                                            bias=RMS_EPS, scale=1.0 / D) for (c0, cw, pc) in subs(T)])
            ch.run(V, lambda: V.reciprocal(out=rstd[:], in_=rstd[:]))

        def prep(xsrc, l, jsh, jsc):
            with ExitStack() as ph:
                xcs = [ph.enter_context(nc.sbuf_tensor(U("p_xc"), [128, T], F32)) for _ in range(NL)]
                sqs = [ph.enter_context(nc.sbuf_tensor(U("p_sq"), [128, T], F32)) for _ in range(NL)]
                rstd = ph.enter_context(nc.sbuf_tensor(U("p_rstd"), [128, T], F32))
                hbs = [ph.enter_context(nc.sbuf_tensor(U("p_hb"), [128, T], BF16)) for _ in range(NL)]
                stats_rms(xsrc, xcs, sqs, rstd)
                cs = ch.fork(NL)
                for kc in range(KC):
                    c, xc, sq, hb = cs[kc % NL], xcs[kc % NL], sqs[kc % NL], hbs[kc % NL]
                    c.run(GP, lambda: GP.dma_start(out=xc[:], in_=xsrc[kc * 128:(kc + 1) * 128, :]), dma=True)
                    c.run(V, lambda: V.tensor_tensor(out=sq[:], in0=xc[:], in1=rstd[:], op=ALU.mult))
                    c.run(S, lambda: [S.activation(out=hb[:, a:b], in_=sq[:, a:b], func=AF.Identity,
                                                   bias=A(l, jsh, kc, r), scale=A(l, jsc, kc, r))
                                      for (a, b, r) in CG])
                    c.run(SY, lambda: SY.dma_start(out=H[kc * 128:(kc + 1) * 128, :], in_=hb[:]), dma=True)
                ch.join(cs)

        def gemm_pipe(wt, m0, M, kcn, act, segw, kind, NW=3, dst=None, bias_off=None, xin=None, xout=None,
                      l=None, jg=None):
            with ExitStack() as ph:
                EP = ph.enter_context
                actb = EP(nc.sbuf_tensor(U("g_act"), [128, kcn, segw], BF16))
                wr = [EP(nc.sbuf_tensor(U("g_w"), [128, kcn * 128], BF16)) for _ in range(NW)]
                ot = [EP(nc.sbuf_tensor(U("g_ot"), [128, segw], F32)) for _ in range(2)]
                if kind == "resid":
                    xi = [EP(nc.sbuf_tensor(U("g_xi"), [128, segw], F32)) for _ in range(2)]
                    o2 = [EP(nc.sbuf_tensor(U("g_o2"), [128, segw], F32)) for _ in range(2)]
                if kind == "qkv":
                    ob = [EP(nc.sbuf_tensor(U("g_ob"), [128, segw], BF16)) for _ in range(2)]
                PD = EP(nc.semaphore(U("PD")))
                EV = EP(nc.semaphore(U("EV")))
                WF = EP(nc.semaphore(U("WF")))
                AL = EP(nc.semaphore(U("AL")))
                OD = EP(nc.semaphore(U("OD")))
                XF = EP(nc.semaphore(U("XF")))
                VD = EP(nc.semaphore(U("VD")))
                for eng in (GP, SY, PE, S, V):
                    ch.wait_all(eng)
                act3 = act.rearrange("(kc p) t -> p kc t", p=128)
                items = [(si, s0, sw, m) for si, (s0, sw) in enumerate(segs(T, segw)) for m in range(M)]
                od_after = []
                n_od = 0
                for i, (si, s0, sw, m) in enumerate(items):
                    p = i % 2
                    wbuf = wr[i % NW]
                    pb = p * 1536
                    if m == 0:
                        if si > 0:
                            SY.wait_ge(PD, i)
                        SY.dma_start(out=actb[:, :, 0:sw], in_=act3[:, :, s0:s0 + sw]).then_inc(AL, 16)
                    if i >= NW:
                        GP.wait_ge(PD, i - NW + 1)
                    GP.dma_start(out=wbuf[:], in_=wt[m0 + m]).then_inc(WF, 16)
                    if kind == "resid":
                        if i >= 2:
                            GP.wait_ge(VD, i - 1)
                        GP.dma_start(out=xi[p][:, 0:sw], in_=xin[m * 128:(m + 1) * 128, s0:s0 + sw]).then_inc(XF, 16)
                    PE.wait_ge(WF, 16 * (i + 1))
                    if m == 0:
                        PE.wait_ge(AL, 16 * (si + 1))
                    if i >= 2:
                        PE.wait_ge(EV, i - 1)
                    mm = [PE.matmul(psF[:, pb + pc:pb + pc + cw], lhsT=wbuf[:, kc * 128:(kc + 1) * 128],
                                    rhs=actb[:, kc, c0:c0 + cw], start=(kc == 0), stop=(kc == kcn - 1))
                          for (c0, cw, pc) in subs(sw) for kc in range(kcn)]
                    mm[-1].then_inc(PD, 1)
                    S.wait_ge(PD, i + 1)
                    if i >= 2:
                        if kind == "resid":
                            S.wait_ge(VD, i - 1)
                        else:
                            S.wait_ge(OD, 16 * od_after[i - 2])
                    bias = 0.0 if bias_off is None else vcol(bias_off + m)
                    ev = [S.activation(out=ot[p][:, c0:c0 + cw], in_=psF[:, pb + pc:pb + pc + cw], func=AF.Identity,
                                       bias=bias, scale=1.0) for (c0, cw, pc) in subs(sw)]
                    if kind == "qkv":
                        ev += [S.activation(out=ob[p][:, c0:c0 + cw], in_=psF[:, pb + pc:pb + pc + cw],
                                            func=AF.Identity, bias=0.0, scale=1.0) for (c0, cw, pc) in subs(sw)]
                    ev[-1].then_inc(EV, 1)
                    if kind == "store":
                        SY.wait_ge(EV, i + 1)
                        SY.dma_start(out=dst[m * 128:(m + 1) * 128, s0:s0 + sw], in_=ot[p][:, 0:sw]).then_inc(OD, 16)
                        n_od += 1
                    elif kind == "resid":
                        V.wait_ge(EV, i + 1)
                        V.wait_ge(XF, 16 * (i + 1))
                        if i >= 2:
                            V.wait_ge(OD, 16 * (i - 1))
                        ops = []
                        for (a, b, r) in CG:
                            lo, hi = max(a, s0), min(b, s0 + sw)
                            if lo < hi:
                                ops.append((lo - s0, hi - s0, r))
                        vi = [V.scalar_tensor_tensor(out=o2[p][:, a:b], in0=ot[p][:, a:b], scalar=A(l, jg, m, r),
                                                     in1=xi[p][:, a:b], op0=ALU.mult, op1=ALU.add)
                              for (a, b, r) in ops]
                        vi[-1].then_inc(VD, 1)
                        SY.wait_ge(VD, i + 1)
                        SY.dma_start(out=xout[m * 128:(m + 1) * 128, s0:s0 + sw], in_=o2[p][:, 0:sw]).then_inc(OD, 16)
                        n_od += 1
                    else:
                        g, rem = divmod(m, 48)
                        j, h = divmod(rem, 16)
                        SY.wait_ge(EV, i + 1)
                        SY.dma_start(out=QKVb[m * 128:(m + 1) * 128, s0:s0 + sw], in_=ob[p][:, 0:sw]).then_inc(OD, 16)
                        n_od += 1
                        if j > 0:
                            frow = (j - 1) * 2048 + h * 128
                            lo, hi = max(P - WB[g], s0), min(P, s0 + sw)
                            if lo < hi:
                                SY.dma_start(out=kvpT[g][frow:frow + 128, lo - (P - WB[g]):hi - (P - WB[g])],
                                             in_=ot[p][:, lo - s0:hi - s0]).then_inc(OD, 16)
                                n_od += 1
                            if s0 + sw == T:
                                for s in range(NS):
                                    c = P + TS * s - s0
                                    SY.dma_start(out=kvs_newT[s, g, frow:frow + 128, :],
                                                 in_=ot[p][:, c:c + TS]).then_inc(OD, 16)
                                    SY.dma_start(out=KVT[g][s, frow:frow + 128, WB[g]:WB[g] + TS],
                                                 in_=ob[p][:, c:c + TS]).then_inc(OD, 16)
                                    n_od += 2
                    od_after.append(n_od)
                ch.prevs = [(OD, 16 * n_od)]

        def gemm_up(l):
            wt, m0, kcn, segw, NW = w_up_t, l * 172, KC, 1032, 3
            with ExitStack() as ph:
                EP = ph.enter_context
                actb = EP(nc.sbuf_tensor(U("u_act"), [128, kcn, segw], BF16))
                wr = [EP(nc.sbuf_tensor(U("u_w"), [128, kcn * 128], BF16)) for _ in range(NW)]
                ot = [EP(nc.sbuf_tensor(U("u_ot"), [128, segw + 2], F32)) for _ in range(4)]
                cgb = [EP(nc.sbuf_tensor(U("u_cg"), [128, segw], F32)) for _ in range(2)]
                cvb = [EP(nc.sbuf_tensor(U("u_cv"), [128, segw], F32)) for _ in range(2)]
                gbb = [EP(nc.sbuf_tensor(U("u_gb"), [128, segw], BF16)) for _ in range(2)]
                carry = EP(nc.sbuf_tensor(U("u_carry"), [128, 172, 2], F32))
                sst = [[EP(nc.sbuf_tensor(U("u_ss"), [128, NS, TS + 2], F32)) for _ in range(2)] for _ in range(2)]
                PD, EV, WF, AL, OD, XF, CD, SD, VD, GD = (EP(nc.semaphore(U(n))) for n in
                                                          ("PD", "EV", "WF", "AL", "OD", "XF", "CD", "SD", "VD", "GD"))
                dc = Chain(nc, ph)
                for eng in (GP, SY, PE, S, V):
                    ch.wait_all(eng)
                dc.run(V, lambda: [V.memset(o[:, 0:2], 0.0) for o in ot])
                act3 = H.rearrange("(kc p) t -> p kc t", p=128)
                items = [(si, s0, sw, f, half) for si, (s0, sw) in enumerate(segs(T, segw))
                         for f in range(FC) for half in range(2)]
                od_after = []
                n_od = 0
                n_xf = 0
                for i, (si, s0, sw, f, half) in enumerate(items):
                    m = f + FC * half
                    q = i // 2
                    p = i % 2
                    pb = p * 1536
                    wbuf = wr[i % NW]
                    o = ot[i % 4]
                    last_seg = (s0 + sw == T)
                    npr = P - s0 if last_seg else sw
                    if f == 0 and half == 0:
                        if si > 0:
                            SY.wait_ge(PD, i)
                        SY.dma_start(out=actb[:, :, 0:sw], in_=act3[:, :, s0:s0 + sw]).then_inc(AL, 16)
                    if i >= NW:
                        GP.wait_ge(PD, i - NW + 1)
                    GP.dma_start(out=wbuf[:], in_=wt[m0 + m]).then_inc(WF, 16)
                    if last_seg and half == 0:
                        if q >= 2:
                            GP.wait_ge(CD, q - 1)
                        for hh in range(2):
                            row = (f + FC * hh) * 128
                            for s in range(NS):
                                GP.dma_start(out=sst[q % 2][hh][:, s, 0:2],
                                             in_=sffnT[l, s, row:row + 128, :]).then_inc(XF, 16)
                                n_xf += 1
                    PE.wait_ge(WF, 16 * (i + 1))
                    if f == 0 and half == 0:
                        PE.wait_ge(AL, 16 * (si + 1))
                    if i >= 2:
                        PE.wait_ge(EV, i - 1)
                    mm = [PE.matmul(psF[:, pb + pc:pb + pc + cw], lhsT=wbuf[:, kc * 128:(kc + 1) * 128],
                                    rhs=actb[:, kc, c0:c0 + cw], start=(kc == 0), stop=(kc == kcn - 1))
                          for (c0, cw, pc) in subs(sw) for kc in range(kcn)]
                    mm[-1].then_inc(PD, 1)
                    S.wait_ge(PD, i + 1)
                    if i >= 4:
                        S.wait_ge(CD, q - 1)
                        S.wait_ge(OD, 16 * od_after[i - 4])
                    ev = [S.activation(out=o[:, 2 + c0:2 + c0 + cw], in_=psF[:, pb + pc:pb + pc + cw],
                                       func=AF.Identity, bias=0.0, scale=1.0) for (c0, cw, pc) in subs(sw)]
                    ev[-1].then_inc(EV, 1)
                    if last_seg:
                        SY.wait_ge(EV, i + 1)
                        SY.dma_start(out=ffnpT[l, m * 128:(m + 1) * 128, :], in_=o[:, npr:npr + 2]).then_inc(OD, 16)
                        for s in range(NS):
                            c = 2 + npr + TS * s + TS - 2
                            SY.dma_start(out=ffnsT[l, s, m * 128:(m + 1) * 128, :], in_=o[:, c:c + 2]).then_inc(OD, 16)
                        n_od += 3
                    od_after.append(n_od)
                    if half == 0:
                        continue
                    V.wait_ge(EV, i + 1)
                    if q >= 2:
                        V.wait_ge(VD, q - 1)
                    if last_seg:
                        V.wait_ge(XF, 16 * n_xf)
                    for hh in range(2):
                        oo = ot[(i - 1 + hh) % 4]
                        mm_ = f + FC * hh
                        res = (cgb if hh == 0 else cvb)[q % 2]
                        wk_ = [vcol(V_WFD + (l * 3 + k) * 172 + mm_) for k in range(3)]
                        if not last_seg:
                            dc.run(V, lambda: V.tensor_copy(out=carry[:, mm_, :], in_=oo[:, sw:sw + 2]))
                        else:
                            dc.run(V, lambda: V.tensor_copy(out=oo[:, 0:2], in_=carry[:, mm_, :]))
                        dc.run(V, lambda: V.tensor_scalar(out=res[:, 0:sw], in0=oo[:, 0:sw], scalar1=wk_[0], scalar2=None,
                                                          op0=ALU.mult))
                        for k in (1, 2):
                            dc.run(V, lambda: V.scalar_tensor_tensor(out=res[:, 0:sw], in0=oo[:, k:sw + k], scalar=wk_[k],
                                                                     in1=res[:, 0:sw], op0=ALU.mult, op1=ALU.add))
                        if last_seg:
                            bs = sst[q % 2][hh]
                            rs = res[:, npr:npr + NS * TS].rearrange("p (s t) -> p s t", t=TS)
                            dc.run(V, lambda: V.tensor_copy(
                                out=bs[:, :, 2:TS + 2],
                                in_=oo[:, 2 + npr:2 + npr + NS * TS].rearrange("p (s t) -> p s t", t=TS)))
                            dc.run(V, lambda: V.tensor_scalar(out=rs, in0=bs[:, :, 0:TS], scalar1=wk_[0], scalar2=None,
                                                              op0=ALU.mult))
                            for k in (1, 2):
                                dc.run(V, lambda: V.scalar_tensor_tensor(out=rs, in0=bs[:, :, k:TS + k], scalar=wk_[k],
                                                                         in1=rs, op0=ALU.mult, op1=ALU.add))
                    V.wait_ge(dc.prev[0], dc.prev[1])
                    V.engine_nop().then_inc(CD, 1)
                    cg, cv, gb = cgb[q % 2], cvb[q % 2], gbb[q % 2]
                    S.wait_ge(CD, q + 1)
                    S.activation(out=cg[:, 0:sw], in_=cg[:, 0:sw], func=AF.Silu).then_inc(SD, 1)
                    V.wait_ge(SD, q + 1)
                    if q >= 2:
                        V.wait_ge(GD, 16 * (q - 1))
                    V.tensor_tensor(out=gb[:, 0:sw], in0=cg[:, 0:sw], in1=cv[:, 0:sw], op=ALU.mult).then_inc(VD, 1)
                    SY.wait_ge(VD, q + 1)
                    SY.dma_start(out=G[f * 128:(f + 1) * 128, s0:s0 + sw], in_=gb[:, 0:sw]).then_inc(GD, 16)
                npairs = len(items) // 2
                ch.prevs = [(GD, 16 * npairs), (OD, 16 * n_od)]

        def gemm_store(wt, m0, M, kcn, act, segw, dst, bias_off=None):
            gemm_pipe(wt, m0, M, kcn, act, segw, "store", dst=dst, bias_off=bias_off)

        def gemm_resid(wt, m0, M, kcn, act, segw, xin, xout, l, jg, bias_off=None):
            gemm_pipe(wt, m0, M, kcn, act, segw, "resid", NW=(2 if kcn > 32 else 3), xin=xin, xout=xout, l=l, jg=jg,
                      bias_off=bias_off)

        prep(xT, 0, 0, 1)

        for g in range(3):
            ch.run(SY, lambda: [SY.dma_start(out=kvs_old[g][s], in_=ck[g][s, 8:WB[g], :]) for s in range(NS)],
                   dma=True)
        with ExitStack() as ph:
            cins = [ph.enter_context(nc.sbuf_tensor(U("ct_in"), [128, 4096], F32)) for _ in range(2)]
            couts = [ph.enter_context(nc.sbuf_tensor(U("ct_out"), [128, 32, 128], BF16)) for _ in range(2)]
            cs = ch.fork(2)
            it = 0
            for g in range(3):
                for s in range(NS):
                    kv3 = KVT[g][s].rearrange("(c p) w -> p c w", p=128)
                    for rt in range(WB[g] // 128):
                        c, cin, cout = cs[it % 2], cins[it % 2], couts[it % 2]
                        pc = (it % 2) * 512
                        it += 1
                        c.run(GP, lambda: GP.dma_start(out=cin[:], in_=ck[g][s, rt * 128:(rt + 1) * 128, :]),
                              dma=True)
                        for q4 in range(8):
                            c.run(PE, lambda: [PE.transpose(psF[:, pc + i * 128:pc + (i + 1) * 128],
                                                            cin[:, (q4 * 4 + i) * 128:(q4 * 4 + i + 1) * 128],
                                                            identf[:]) for i in range(4)])
                            c.run(S, lambda: S.copy(out=cout[:, q4 * 4:(q4 + 1) * 4, :],
                                                    in_=psF[:, pc:pc + 512].rearrange("p (c w) -> p c w", w=128)))
                        c.run(SY, lambda: SY.dma_start(out=kv3[:, :, rt * 128:(rt + 1) * 128], in_=cout[:]),
                              dma=True)
            ch.join(cs)

        gemm_pipe(w_qkv_t, 0, 144, KC, H, 1032, "qkv")

        with ExitStack() as ph:
            EP = ph.enter_context
            hbufs = []
            for _hb in range(2):
                hbufs.append(dict(
                    qT=EP(nc.sbuf_tensor(U("a_q"), [128, T], BF16)),
                    kT=EP(nc.sbuf_tensor(U("a_k"), [128, T], BF16)),
                    vT=EP(nc.sbuf_tensor(U("a_v"), [128, T], BF16)),
                    ksb=[EP(nc.sbuf_tensor(U(f"a_ks{s}"), [128, WPAD[2]], BF16)) for s in range(NS)],
                    vsb=[EP(nc.sbuf_tensor(U(f"a_vs{s}"), [128, WPAD[2]], BF16)) for s in range(NS)]))
            lc = Chain(nc, ph)

            def head_loads(n):
                g_, h_ = divmod(n, 16)
                hb_ = hbufs[n % 2]
                mq_, mk_, mv_ = g_ * 48 + h_, g_ * 48 + 16 + h_, g_ * 48 + 32 + h_
                wk_ = WB[g_] + TS
                lc.run(GP, lambda: [GP.dma_start(out=hb_["qT"][:], in_=QKVb[mq_ * 128:(mq_ + 1) * 128, :]),
                                    GP.dma_start(out=hb_["kT"][:], in_=QKVb[mk_ * 128:(mk_ + 1) * 128, :]),
                                    GP.dma_start(out=hb_["vT"][:], in_=QKVb[mv_ * 128:(mv_ + 1) * 128, :])] +
                       [ins for s in range(NS) for ins in (
                           GP.dma_start(out=hb_["ksb"][s][:, 0:wk_], in_=KVT[g_][s, h_ * 128:(h_ + 1) * 128, 0:wk_]),
                           GP.dma_start(out=hb_["vsb"][s][:, 0:wk_],
                                        in_=KVT[g_][s, 2048 + h_ * 128:2048 + (h_ + 1) * 128, 0:wk_]))], dma=True)
            Vall = EP(nc.sbuf_tensor(U("a_vall"), [128, 48, 128], BF16))

            def mkset(p):
                d_ = {}
                d_["Sm"] = EP(nc.sbuf_tensor(U("a_sm"), [128, 4, 256], F32))
                d_["Pb"] = EP(nc.sbuf_tensor(U("a_p"), [128, 4, 256], BF16))
                d_["PTs"] = EP(nc.sbuf_tensor(U("a_pt"), [128, 4, 2, 128], BF16))
                d_["OUT"] = EP(nc.sbuf_tensor(U("a_out"), [128, 4, 129], F32))
                for nm in ("mx", "ngm", "den", "lnd", "rden"):
                    d_[nm] = EP(nc.sbuf_tensor(U("a_" + nm), [128, 4], F32))
                d_["psS"] = psF[:, p * 1536:p * 1536 + 1024].rearrange("p (u w) -> p u w", w=256)
                d_["psO"] = psF[:, p * 1536 + 1024:p * 1536 + 1536].rearrange("p (u w) -> p u w", w=128)
                d_["psPT"] = psB[:, p * 1024:(p + 1) * 1024].rearrange("p (u i w) -> p u i w", i=2, w=128)
                return d_
            sets = [mkset(0), mkset(1)]

            def attn_batch(c, bs, units, Mq, nk2, g, h):
                NB = len(units)
                W = 128 + nk2
                Sm, Pb, PTs, OUT = bs["Sm"], bs["Pb"], bs["PTs"], bs["OUT"]
                mx, ngm, den, lnd, rden = bs["mx"], bs["ngm"], bs["den"], bs["lnd"], bs["rden"]
                psS, psO, psPT = bs["psS"], bs["psO"], bs["psPT"]
                c.run(PE, lambda: [ins for u, un in enumerate(units) for ins in (
                    PE.matmul(psS[:Mq, u, 0:128], lhsT=un["q"], rhs=un["k1"], start=True, stop=True),
                    PE.matmul(psS[:Mq, u, 128:W], lhsT=un["q"], rhs=un["k2"], start=True, stop=True))])
                yield
                c.run(V, lambda: [V.scalar_tensor_tensor(out=Sm[:Mq, u, 0:W], in0=psS[:Mq, u, 0:W], scalar=SCALE,
                                                         in1=un["mask"], op0=ALU.mult, op1=ALU.add)
                                  for u, un in enumerate(units)])
                yield
                c.run(V, lambda: V.tensor_reduce(out=mx[:Mq, 0:NB], in_=Sm[:Mq, 0:NB, 0:W], axis=AX.X, op=ALU.max))
                c.run(V, lambda: V.tensor_scalar_mul(out=ngm[:Mq, 0:NB], in0=mx[:Mq, 0:NB], scalar1=-1.0))
                c.run(V, lambda: V.memset(den[:Mq, 0:NB], 0.0))
                yield
                for u in range(NB):
                    c.run(S, lambda: S.activation(out=Pb[:Mq, u, 0:W], in_=Sm[:Mq, u, 0:W], func=AF.Exp,
                                                  bias=ngm[:Mq, u:u + 1], scale=1.0, accum_out=den[:Mq, u:u + 1]))
                yield
                c.run(PE, lambda: [ins for u in range(NB) for ins in (
                    PE.transpose(psPT[:128, u, 0, 0:Mq], Pb[:Mq, u, 0:128], identb[:Mq, :Mq]),
                    PE.transpose(psPT[:nk2, u, 1, 0:Mq], Pb[:Mq, u, 128:W], identb[:Mq, :Mq]))])
                yield
                c.run(S, lambda: [S.copy(out=PTs[:128, 0:NB, 0, 0:Mq], in_=psPT[:128, 0:NB, 0, 0:Mq]),
                                  S.copy(out=PTs[:nk2, 0:NB, 1, 0:Mq], in_=psPT[:nk2, 0:NB, 1, 0:Mq])])
                yield
                c.run(PE, lambda: [ins for u, un in enumerate(units) for ins in (
                    PE.matmul(psO[:Mq, u, :], lhsT=PTs[:128, u, 0, 0:Mq], rhs=Vall[:128, un["t1"], :],
                              start=True, stop=False),
                    PE.matmul(psO[:Mq, u, :], lhsT=PTs[:nk2, u, 1, 0:Mq], rhs=Vall[:nk2, un["t2"], :],
                              start=False, stop=True))])
                yield
                c.run(S, lambda: S.activation(out=lnd[:Mq, 0:NB], in_=den[:Mq, 0:NB], func=AF.Ln))
                c.run(V, lambda: V.tensor_tensor(out=OUT[:Mq, 0:NB, 128], in0=mx[:Mq, 0:NB], in1=lnd[:Mq, 0:NB],
                                                 op=ALU.add))
                c.run(V, lambda: V.reciprocal(out=rden[:Mq, 0:NB], in_=den[:Mq, 0:NB]))
                c.run(V, lambda: [V.tensor_scalar(out=OUT[:Mq, u, 0:128], in0=psO[:Mq, u, :],
                                                  scalar1=rden[:Mq, u:u + 1], scalar2=None, op0=ALU.mult)
                                  for u in range(NB)])
                yield
                c.run(SY, lambda: [SY.dma_start(out=OATT[un["rows"], g, h, :], in_=OUT[:Mq, u, :])
                                   for u, un in enumerate(units)], dma=True)
                yield

            def lane(c, bs, batches):
                for (units, Mq, nk2, g, h) in batches:
                    yield from attn_batch(c, bs, units, Mq, nk2, g, h)

            master = msk[:, 0:256]
            maskF = msk[:, 256:512]
            lc.prevs = list(ch.prevs)
            head_loads(0)
            for g in range(3):
                d = DIL[g]
                nb = (P // d) // 128
                for h in range(16):
                    n_ = g * 16 + h
                    hb = hbufs[n_ % 2]
                    qT, kT, vT, ksb, vsb = hb["qT"], hb["kT"], hb["vT"], hb["ksb"], hb["vsb"]
                    done_prev = list(ch.prevs)
                    ch.prevs = done_prev + list(lc.prevs)
                    if n_ + 1 < 48:
                        lc.prevs = done_prev
                        head_loads(n_ + 1)
                    vtiles = []

                    def vt(ap, nk):
                        vtiles.append((ap, nk))
                        return len(vtiles) - 1
                    batches = []
                    units = []
                    blk = {}
                    for r in range(d):
                        for j in range(nb):
                            st = r + d * 128 * j
                            blk[(r, j)] = (slice(st, st + d * 127 + 1, d), vt(vT[:, st:st + d * 127 + 1:d], 128))
                    for r in range(d):
                        for j in range(nb):
                            cur, tcur = blk[(r, j)]
                            prv, tprv = blk[(r, j - 1)] if j > 0 else blk[(r, j)]
                            units.append(dict(q=qT[:, cur], k1=kT[:, prv], t1=tprv, k2=kT[:, cur], t2=tcur,
                                              mask=(master if j > 0 else maskF), rows=cur))
                    for b0 in range(0, len(units), 4):
                        batches.append((units[b0:b0 + 4], 128, 128, g, h))
                    units = []
                    if g == 0:
                        Mq, nk2 = 8, 8
                        for s in range(NS):
                            q0 = P + TS * s
                            units.append(dict(q=qT[:, q0:q0 + 8], k1=ksb[s][:, 0:128], t1=vt(vsb[s][:, 0:128], 128),
                                              k2=ksb[s][:, 128:136], t2=vt(vsb[s][:, 128:136], 8),
                                              mask=master[:8, 0:136], rows=slice(q0, q0 + 8)))
                    elif g == 1:
                        Mq, nk2 = 2, 2
                        for s in range(NS):
                            for r in range(4):
                                q0 = P + TS * s + r
                                units.append(dict(q=qT[:, q0:q0 + 5:4], k1=ksb[s][:, r:r + 509:4],
                                                  t1=vt(vsb[s][:, r:r + 509:4], 128),
                                                  k2=ksb[s][:, 512 + r:512 + r + 5:4],
                                                  t2=vt(vsb[s][:, 512 + r:512 + r + 5:4], 2),
                                                  mask=master[:2, 0:130], rows=slice(q0, q0 + 5, 4)))
                    else:
                        Mq, nk2 = 1, 1
                        for s in range(NS):
                            for r in range(8):
                                q0 = P + TS * s + r
                                units.append(dict(q=qT[:, q0:q0 + 1], k1=ksb[s][:, r:r + 2033:16],
                                                  t1=vt(vsb[s][:, r:r + 2033:16], 128),
                                                  k2=ksb[s][:, 2048 + r:2048 + r + 1],
                                                  t2=vt(vsb[s][:, 2048 + r:2048 + r + 1], 1),
                                                  mask=master[:1, 0:129], rows=slice(q0, q0 + 1)))
                    for b0 in range(0, len(units), 4):
                        batches.append((units[b0:b0 + 4], Mq, nk2, g, h))
                    assert len(vtiles) <= 48
                    for r0 in range(0, len(vtiles), 16):
                        rnd = vtiles[r0:r0 + 16]
                        ch.run(PE, lambda: [PE.transpose(psB[:nk, i * 128:(i + 1) * 128], ap, identb[:])
                                            for i, (ap, nk) in enumerate(rnd)])
                        ch.run(V, lambda: [V.tensor_copy(
                            out=Vall[:, r0 + b8:r0 + min(b8 + 8, len(rnd)), :],
                            in_=psB[:, b8 * 128:min(b8 + 8, len(rnd)) * 128].rearrange("p (t w) -> p t w", w=128))
                            for b8 in range(0, len(rnd), 8)])
                    cs = ch.fork(2)
                    lanes = [lane(cs[0], sets[0], batches[0::2]), lane(cs[1], sets[1], batches[1::2])]
                    while lanes:
                        for ln in list(lanes):
                            try:
                                next(ln)
                            except StopIteration:
                                lanes.remove(ln)
                    ch.join(cs)

        with ExitStack() as ph:
            EP = ph.enter_context
            oas = [EP(nc.sbuf_tensor(U("c_oa"), [128, 3, 16, 129], F32)) for _ in range(2)]
            lms = [EP(nc.sbuf_tensor(U("c_lm"), [128, 16], F32)) for _ in range(2)]
            exs = [EP(nc.sbuf_tensor(U("c_ex"), [128, 3, 16], F32)) for _ in range(2)]
            sms = [EP(nc.sbuf_tensor(U("c_sm"), [128, 16], F32)) for _ in range(2)]
            ocs = [EP(nc.sbuf_tensor(U("c_oc"), [128, 16, 128], F32)) for _ in range(2)]
            obs = [EP(nc.sbuf_tensor(U("c_ob"), [128, 16, 128], BF16)) for _ in range(2)]
            at3 = AT.rearrange("(c p) t -> p c t", p=128)
            cs = ch.fork(2)
            for it, (t0, np_) in enumerate(segs(T, 128)):
                c = cs[it % 2]
                oa, lm, ex, sm, oc, ob = (x[it % 2] for x in (oas, lms, exs, sms, ocs, obs))
                pc = (it % 2) * 512
                c.run(GP, lambda: GP.dma_start(out=oa[:np_], in_=OATT[t0:t0 + np_]), dma=True)
                c.run(V, lambda: V.tensor_tensor(out=lm[:np_], in0=oa[:np_, 0, :, 128], in1=oa[:np_, 1, :, 128], op=ALU.max))
                c.run(V, lambda: V.tensor_tensor(out=lm[:np_], in0=lm[:np_], in1=oa[:np_, 2, :, 128], op=ALU.max))
                for g in range(3):
                    c.run(V, lambda: V.tensor_tensor(out=ex[:np_, g, :], in0=oa[:np_, g, :, 128], in1=lm[:np_],
                                                     op=ALU.subtract))
                c.run(S, lambda: S.activation(out=ex[:np_], in_=ex[:np_], func=AF.Exp))
                c.run(V, lambda: V.tensor_tensor(out=sm[:np_], in0=ex[:np_, 0, :], in1=ex[:np_, 1, :], op=ALU.add))
                c.run(V, lambda: V.tensor_tensor(out=sm[:np_], in0=sm[:np_], in1=ex[:np_, 2, :], op=ALU.add))
                c.run(V, lambda: V.reciprocal(out=sm[:np_], in_=sm[:np_]))
                for g in range(3):
                    c.run(V, lambda: V.tensor_tensor(out=ex[:np_, g, :], in0=ex[:np_, g, :], in1=sm[:np_], op=ALU.mult))
                for hh in range(16):
                    c.run(V, lambda: V.tensor_scalar(out=oc[:np_, hh, :], in0=oa[:np_, 0, hh, 0:128],
                                                     scalar1=ex[:np_, 0, hh:hh + 1], scalar2=None, op0=ALU.mult))
                    for g in (1, 2):
                        c.run(V, lambda: V.scalar_tensor_tensor(out=oc[:np_, hh, :], in0=oa[:np_, g, hh, 0:128],
                                                                scalar=ex[:np_, g, hh:hh + 1], in1=oc[:np_, hh, :],
                                                                op0=ALU.mult, op1=ALU.add))
                for q4 in range(4):
                    c.run(PE, lambda: [PE.transpose(psF[:, pc + i * 128:pc + i * 128 + np_], oc[:np_, q4 * 4 + i, :],
                                                    identf[:np_, :np_]) for i in range(4)])
                    c.run(S, lambda: S.copy(out=ob[:, q4 * 4:(q4 + 1) * 4, 0:np_],
                                            in_=psF[:, pc:pc + 512].rearrange("p (c w) -> p c w", w=128)[:, :, 0:np_]))
                c.run(SY, lambda: SY.dma_start(out=at3[:, :, t0:t0 + np_], in_=ob[:, :, 0:np_]), dma=True)
            ch.join(cs)

        gemm_resid(w_o_t, 0, 32, 16, AT, 1032, xT, XA, 0, 2)

        def ffn(l, xin, xout):
            prep(xin, l, 3, 4)
            gemm_up(l)
            gemm_resid(w_down_t, l * 32, 32, FC, G, DOWN_SEGW, xin, xout, l, 5)

        ffn(0, XA, XB)

        prep(XB, 1, 0, 1)
        gemm_store(w_pw1_t, 0, 64, KC, H, 1032, PW1, bias_off=V_BPW1)
        with ExitStack() as ph:
            EP = ph.enter_context

            def two(name, shape, dt=F32):
                return [EP(nc.sbuf_tensor(U(name), shape, dt)) for _ in range(2)]
            a_s, g_s = two("v_a", [128, T]), two("v_g", [128, T])
            ups, uss = two("v_up", [128, P + 30]), two("v_us", [128, NS, TS + 30])
            yts = two("v_y", [128, T])
            mean = EP(nc.sbuf_tensor(U("v_mean"), [128, T], F32))
            rstd = EP(nc.sbuf_tensor(U("v_rstd"), [128, T], F32))
            hbs = two("v_hb", [128, T], BF16)
            ch.run(V, lambda: [V.memset(x[:, 0:30], 0.0) for x in ups])
            cs = ch.fork(2)
            for kc in range(KC):
                c = cs[kc % 2]
                a_, g_, up, us, yt = (x[kc % 2] for x in (a_s, g_s, ups, uss, yts))
                ys = yt[:, P:T].rearrange("p (s t) -> p s t", t=TS)
                c.run(GP, lambda: [GP.dma_start(out=a_[:], in_=PW1[kc * 128:(kc + 1) * 128, :]),
                                   GP.dma_start(out=g_[:], in_=PW1[D + kc * 128:D + (kc + 1) * 128, :]),
                                   GP.dma_start(out=us[:, 0, 0:30], in_=sconvT[0, kc * 128:(kc + 1) * 128, :]),
                                   GP.dma_start(out=us[:, 1, 0:30], in_=sconvT[1, kc * 128:(kc + 1) * 128, :])],
                      dma=True)
                c.run(S, lambda: S.activation(out=g_[:], in_=g_[:], func=AF.Sigmoid))
                c.run(V, lambda: [V.tensor_tensor(out=up[:, 30:P + 30], in0=a_[:, 0:P], in1=g_[:, 0:P], op=ALU.mult),
                                  V.tensor_tensor(out=us[:, :, 30:TS + 30],
                                                  in0=a_[:, P:T].rearrange("p (s t) -> p s t", t=TS),
                                                  in1=g_[:, P:T].rearrange("p (s t) -> p s t", t=TS), op=ALU.mult)])
                c.run(SY, lambda: [SY.dma_start(out=convpT[kc * 128:(kc + 1) * 128, :], in_=up[:, P:P + 30]),
                                   SY.dma_start(out=convsT[0, kc * 128:(kc + 1) * 128, :], in_=us[:, 0, TS:TS + 30]),
                                   SY.dma_start(out=convsT[1, kc * 128:(kc + 1) * 128, :], in_=us[:, 1, TS:TS + 30])],
                      dma=True)
                w0 = vcol(V_WDW + kc * 31)
                c.run(V, lambda: [V.tensor_scalar(out=yt[:, 0:P], in0=up[:, 0:P], scalar1=w0, scalar2=vcol(V_BDW + kc),
                                                  op0=ALU.mult, op1=ALU.add),
                                  V.tensor_scalar(out=ys, in0=us[:, :, 0:TS], scalar1=w0, scalar2=vcol(V_BDW + kc),
                                                  op0=ALU.mult, op1=ALU.add)])
                for j in range(1, 31):
                    wj = vcol(V_WDW + kc * 31 + j)
                    c.run(V, lambda: [V.scalar_tensor_tensor(out=yt[:, 0:P], in0=up[:, j:P + j], scalar=wj,
                                                             in1=yt[:, 0:P], op0=ALU.mult, op1=ALU.add),
                                      V.scalar_tensor_tensor(out=ys, in0=us[:, :, j:TS + j], scalar=wj, in1=ys,
                                                             op0=ALU.mult, op1=ALU.add)])
                c.run(PE, lambda: [PE.matmul(psF[:, pc:pc + cw], lhsT=onesf[:], rhs=yt[:, c0:c0 + cw],
                                             start=(kc == 0), stop=(kc == KC - 1)) for (c0, cw, pc) in subs(T)])
                c.run(SY, lambda: SY.dma_start(out=Y[kc * 128:(kc + 1) * 128, :], in_=yt[:]), dma=True)
            ch.join(cs)
            ch.run(S, lambda: [S.activation(out=mean[:, c0:c0 + cw], in_=psF[:, pc:pc + cw], func=AF.Identity,
                                            bias=0.0, scale=1.0 / D) for (c0, cw, pc) in subs(T)])
            cs = ch.fork(2)
            for kc in range(KC):
                c, yt, a_ = cs[kc % 2], yts[kc % 2], a_s[kc % 2]
                c.run(GP, lambda: GP.dma_start(out=yt[:], in_=Y[kc * 128:(kc + 1) * 128, :]), dma=True)
                c.run(V, lambda: V.tensor_tensor(out=yt[:], in0=yt[:], in1=mean[:], op=ALU.subtract))
                c.run(S, lambda: S.activation(out=a_[:], in_=yt[:], func=AF.Square))
                c.run(PE, lambda: [PE.matmul(psF[:, pc:pc + cw], lhsT=onesf[:], rhs=a_[:, c0:c0 + cw],
                                             start=(kc == 0), stop=(kc == KC - 1)) for (c0, cw, pc) in subs(T)])
            ch.join(cs)
            ch.run(S, lambda: [S.activation(out=rstd[:, c0:c0 + cw], in_=psF[:, pc:pc + cw], func=AF.Sqrt,
                                            bias=LN_EPS, scale=1.0 / D) for (c0, cw, pc) in subs(T)])
            ch.run(V, lambda: V.reciprocal(out=rstd[:], in_=rstd[:]))
            cs = ch.fork(2)
            for kc in range(KC):
                c, yt, hb = cs[kc % 2], yts[kc % 2], hbs[kc % 2]
                c.run(GP, lambda: GP.dma_start(out=yt[:], in_=Y[kc * 128:(kc + 1) * 128, :]), dma=True)
                c.run(V, lambda: V.tensor_tensor(out=yt[:], in0=yt[:], in1=mean[:], op=ALU.subtract))
                c.run(V, lambda: V.tensor_tensor(out=yt[:], in0=yt[:], in1=rstd[:], op=ALU.mult))
                c.run(S, lambda: S.activation(out=hb[:], in_=yt[:], func=AF.Silu, bias=vcol(V_LNB + kc),
                                              scale=vcol(V_LNG + kc)))
                c.run(SY, lambda: SY.dma_start(out=H[kc * 128:(kc + 1) * 128, :], in_=hb[:]), dma=True)
            ch.join(cs)
        gemm_resid(w_pw2_t, 0, 32, KC, H, 1032, XB, XA, 1, 2, bias_off=V_BPW2)

        ffn(1, XA, XB)

        with ExitStack() as ph:
            EP = ph.enter_context
            xcs = [EP(nc.sbuf_tensor(U("n_xc"), [128, T], F32)) for _ in range(NL)]
            sqs = [EP(nc.sbuf_tensor(U("n_sq"), [128, T], F32)) for _ in range(NL)]
            rstd = EP(nc.sbuf_tensor(U("n_rstd"), [128, T], F32))
            stats_rms(XB, xcs, sqs, rstd)
            cs = ch.fork(NL)
            for kc in range(KC):
                c, xc, sq = cs[kc % NL], xcs[kc % NL], sqs[kc % NL]
                c.run(GP, lambda: GP.dma_start(out=xc[:], in_=XB[kc * 128:(kc + 1) * 128, :]), dma=True)
                c.run(V, lambda: V.tensor_tensor(out=sq[:], in0=xc[:], in1=rstd[:], op=ALU.mult))
                c.run(V, lambda: V.tensor_scalar(out=sq[:], in0=sq[:], scalar1=vcol(V_FNG + kc), scalar2=None,
                                                 op0=ALU.mult))
                c.run(SY, lambda: SY.dma_start(out=yT[kc * 128:(kc + 1) * 128, :], in_=sq[:]), dma=True)
            ch.join(cs)
        ch.wait_all(SY)
    return nc


def _tile_w(w, mt=128):
    K, N = w.shape
    kc, m = K // 128, N // 128
    return np.ascontiguousarray(w.reshape(kc, 128, m, 128).transpose(2, 1, 0, 3)).reshape(m, 128, kc * 128)


def _fm(v):
    return np.ascontiguousarray(v.reshape(-1, 128).T)


def _host_shared(inp):
    sh = {}
    sh["w_ada_t"] = np.concatenate([_tile_w(inp["w_ada"][l]) for l in range(2)], axis=0)
    sh["w_qkv_t"] = _tile_w(inp["w_qkv"][0])
    sh["w_o_t"] = _tile_w(inp["w_o"][0])
    sh["w_pw1_t"] = _tile_w(inp["w_pw1"][0])
    sh["w_pw2_t"] = _tile_w(inp["w_pw2"][0])
    sh["w_up_t"] = np.concatenate([_tile_w(inp["w_up"][l]) for l in range(2)], axis=0)
    sh["w_down_t"] = np.concatenate([_tile_w(inp["w_down"][l]) for l in range(2)], axis=0)
    vec = np.zeros((128, NV), np.float32)
    for l in range(2):
        vec[:, V_BADA + l * 192:V_BADA + (l + 1) * 192] = _fm(inp["b_ada"][l])
    vec[:, V_BPW1:V_BPW1 + 64] = _fm(inp["b_pw1"][0])
    wdw = inp["w_dw"][0]
    vec[:, V_WDW:V_WDW + 992] = wdw.reshape(31, 32, 128).transpose(2, 1, 0).reshape(128, 992)
    vec[:, V_BDW:V_BDW + 32] = _fm(inp["b_dw"][0])
    vec[:, V_LNG:V_LNG + 32] = _fm(inp["ln_g"][0])
    vec[:, V_LNB:V_LNB + 32] = _fm(inp["ln_b"][0])
    vec[:, V_BPW2:V_BPW2 + 32] = _fm(inp["b_pw2"][0])
    for l in range(2):
        for k in range(3):
            o = V_WFD + (l * 3 + k) * 172
            vec[:, o:o + 172] = _fm(inp["w_ffn_dw"][l, k])
    vec[:, V_FNG:V_FNG + 32] = _fm(inp["final_norm_g"])
    sh["vecs"] = vec
    qi = np.arange(128)[:, None]
    kj = np.arange(256)[None, :]
    valid = (kj >= qi) & (kj <= qi + 128)
    master = np.where(valid, 0.0, NEG).astype(np.float32)
    maskF = master.copy()
    maskF[:, :128] = NEG
    sh["masks"] = np.concatenate([master, maskF], axis=1)
    sh["identf"] = np.eye(128, dtype=np.float32)
    return sh


def _host_core(inp, sh, b):
    m = dict(sh)
    xs = [inp["x_prompt"][b].T] + [inp["x_sample"][2 * b + s].T for s in range(NS)]
    m["xT"] = np.ascontiguousarray(np.concatenate(xs, axis=1))
    cs = np.stack([inp["c_prompt"][b]] + [inp["c_sample"][2 * b + s] for s in range(NS)], axis=0)
    m["cT"] = np.ascontiguousarray(cs.reshape(3, KC, 128).transpose(2, 1, 0)).reshape(128, KC * 3)
    for g, name in enumerate(("cache_kv_g1", "cache_kv_g2", "cache_kv_g3")):
        m[f"ck{g}"] = np.ascontiguousarray(inp[name][0, 2 * b:2 * b + 2].reshape(NS, WB[g], 4096))
    m["sconvT"] = np.ascontiguousarray(inp["state_conv"][0, 2 * b:2 * b + 2].transpose(0, 2, 1))
    m["sffnT"] = np.ascontiguousarray(inp["state_ffn_conv"][:, 2 * b:2 * b + 2].transpose(0, 1, 3, 2))
    return m


def _assemble(results, ncores=NCORES):
    y_p = np.zeros((4, P, D), np.float32)
    y_s = np.zeros((8, TS, D), np.float32)
    kvp = [np.zeros((1, 4, WB[g], 2, 16, 128), np.float32) for g in range(3)]
    convp = np.zeros((1, 4, 30, D), np.float32)
    ffnp = np.zeros((2, 4, 2, 2 * F), np.float32)
    kvs = [np.zeros((1, 8, WB[g], 2, 16, 128), np.float32) for g in range(3)]
    convs = np.zeros((1, 8, 30, D), np.float32)
    ffns = np.zeros((2, 8, 2, 2 * F), np.float32)
    for b in range(ncores):
        r = results[b]
        y_p[b] = r["yT"][:, :P].T
        for s in range(NS):
            y_s[2 * b + s] = r["yT"][:, P + TS * s:P + TS * (s + 1)].T
        for g in range(3):
            kvp[g][0, b] = r[f"kvp{g}T"].T.reshape(WB[g], 2, 16, 128)
            for s in range(NS):
                full = np.concatenate([r[f"kvs_old{g}"][s], r["kvs_newT"][s, g].T], axis=0)
                kvs[g][0, 2 * b + s] = full.reshape(WB[g], 2, 16, 128)
        convp[0, b] = r["convpT"].T
        for l in range(2):
            ffnp[l, b] = r["ffnpT"][l].T
            for s in range(NS):
                ffns[l, 2 * b + s] = r["ffnsT"][l, s].T
        for s in range(NS):
            convs[0, 2 * b + s] = r["convsT"][s].T
    return (y_p, y_s, kvp[0], kvp[1], kvp[2], convp, ffnp, kvs[0], kvs[1], kvs[2], convs, ffns)


def kernel(**inputs):
    inp = {k: np.asarray(v) for k, v in inputs.items()}
    sh = _host_shared(inp)
    in_maps = [_host_core(inp, sh, b) for b in range(NCORES)]
    nc = build_program()
    res = run_bass_kernel_spmd(nc, in_maps, core_ids=list(range(NCORES)))
    return _assemble(res.results)
```
